# Optimizing a Trainium2 kernel written in Bass

```python
import math
import jax, jax.numpy as jnp
from jax import lax
import numpy as np

D_MODEL = 1024
BATCH = 8
SEQ = 2048
DEPTH = 4

SG_WIDTH = 512
SG_GROUPS = 8
SG_GROUP_DIM = SG_WIDTH // SG_GROUPS
CHUNK = 128
CV_WIDTH = 512
CV_KERNEL = 31
SB_HEADS = 8
SB_HEAD_DIM = 64
SB_WIDTH = SB_HEADS * SB_HEAD_DIM
Q_BLOCK = 128
N_BRANCH = 3
D_FF = 2816
FFN_KERNEL = 3

IN_COLS = 2 * SG_WIDTH + 2 * CV_WIDTH + 3 * SB_WIDTH + N_BRANCH * D_MODEL
EPS = 1e-6

kernel_name = "hybrid_sgu_conformer_stickbreak_block"


def rms_norm(x, g):
    xf = x.astype(jnp.float32)
    y = xf * lax.rsqrt(jnp.mean(xf * xf, axis=-1, keepdims=True) + EPS)
    return (y * g.astype(jnp.float32)).astype(x.dtype)


def layer_norm(x, g, b):
    xf = x.astype(jnp.float32)
    mu = jnp.mean(xf, axis=-1, keepdims=True)
    xc = xf - mu
    y = xc * lax.rsqrt(jnp.mean(xc * xc, axis=-1, keepdims=True) + EPS)
    return (y * g.astype(jnp.float32) + b.astype(jnp.float32)).astype(x.dtype)


def causal_depthwise_conv(x, w, b):
    k_width, ch = w.shape
    y = lax.conv_general_dilated(
        x, w[:, None, :].astype(x.dtype), window_strides=(1,), padding=[(k_width - 1, 0)],
        dimension_numbers=("NWC", "WIO", "NWC"), feature_group_count=ch)
    return y + b.astype(x.dtype)


def chunked_spatial_gating(v, w_s, b_s):
    bsz, t, _ = v.shape
    vc = v.reshape(bsz, t // CHUNK, CHUNK, SG_GROUPS, SG_GROUP_DIM)
    tril = jnp.tril(jnp.ones((CHUNK, CHUNK), dtype=bool))
    w_m = jnp.where(tril[None], w_s, jnp.zeros_like(w_s)).astype(v.dtype)
    out = jnp.einsum("gts,bcsgd->bctgd", w_m, vc) + b_s.T.astype(v.dtype)[:, :, None]
    return out.reshape(bsz, t, SG_WIDTH)


def stick_breaking_attention(q, k, v):
    bsz, t, h, dh = q.shape
    q = q.transpose(0, 2, 1, 3)
    k = k.transpose(0, 2, 1, 3)
    v = v.transpose(0, 2, 1, 3)
    scale = 1.0 / math.sqrt(dh)
    outs = []
    for i in range(t // Q_BLOCK):
        ctx = (i + 1) * Q_BLOCK
        q_blk = q[:, :, i * Q_BLOCK:ctx]
        z = jnp.einsum("bhqd,bhkd->bhqk", q_blk, k[:, :, :ctx]).astype(jnp.float32) * scale
        q_pos = i * Q_BLOCK + jnp.arange(Q_BLOCK)
        k_pos = jnp.arange(ctx)
        mask = k_pos[None, :] < q_pos[:, None]
        log_1m_beta = jnp.where(mask, jax.nn.log_sigmoid(-z), 0.0)
        cs = lax.cumsum(log_1m_beta, axis=3)
        log_a = jax.nn.log_sigmoid(z) + (cs[..., -1:] - cs)
        a = jnp.where(mask, jnp.exp(log_a), 0.0)
        outs.append(jnp.einsum("bhqk,bhkd->bhqd", a.astype(v.dtype), v[:, :, :ctx]))
    o = jnp.concatenate(outs, axis=2)
    return o.transpose(0, 2, 1, 3).reshape(bsz, t, h * dh)


def setup_inputs(seed: int = 0) -> dict:
    key = jax.random.key(seed)
    ks = jax.random.split(key, 24)
    f32 = jnp.float32
    L, D = DEPTH, D_MODEL

    def nrm(k, shape, scale):
        return jax.random.normal(k, shape, f32) * scale

    return {
        "x": jax.random.normal(ks[0], (BATCH, SEQ, D), f32),
        "ln1_g": 1.0 + nrm(ks[1], (L, D), 0.05),
        "w_in": nrm(ks[2], (L, D, IN_COLS), D ** -0.5),
        "b_gate": nrm(ks[3], (L, N_BRANCH, D), 0.1),
        "sg_ln_g": 1.0 + nrm(ks[4], (L, SG_WIDTH), 0.05),
        "sg_ln_b": nrm(ks[5], (L, SG_WIDTH), 0.05),
        "sg_w": nrm(ks[6], (L, SG_GROUPS, CHUNK, CHUNK), CHUNK ** -0.5),
        "sg_b": 1.0 + nrm(ks[7], (L, SG_GROUPS, CHUNK), 0.1),
        "w_a_out": nrm(ks[8], (L, SG_WIDTH, D), SG_WIDTH ** -0.5),
        "cv_w": nrm(ks[9], (L, CV_KERNEL, CV_WIDTH), CV_KERNEL ** -0.5),
        "cv_b": nrm(ks[10], (L, CV_WIDTH), 0.05),
        "cv_ln_g": 1.0 + nrm(ks[11], (L, CV_WIDTH), 0.05),
        "cv_ln_b": nrm(ks[12], (L, CV_WIDTH), 0.05),
        "w_b_out": nrm(ks[13], (L, CV_WIDTH, D), CV_WIDTH ** -0.5),
        "q_norm_g": 1.0 + nrm(ks[14], (L, SB_HEAD_DIM), 0.05),
        "k_norm_g": 1.0 + nrm(ks[15], (L, SB_HEAD_DIM), 0.05),
        "w_c_out": nrm(ks[16], (L, SB_WIDTH, D), SB_WIDTH ** -0.5),
        "w_out": nrm(ks[17], (L, D, D), D ** -0.5),
        "ln2_g": 1.0 + nrm(ks[18], (L, D), 0.05),
        "w_up": nrm(ks[19], (L, D, 2 * D_FF), D ** -0.5),
        "ffn_conv_w": nrm(ks[20], (L, FFN_KERNEL, 2 * D_FF), FFN_KERNEL ** -0.5),
        "ffn_conv_b": nrm(ks[21], (L, 2 * D_FF), 0.05),
        "w_down": nrm(ks[22], (L, D_FF, D), D_FF ** -0.5),
    }


def reference(x, ln1_g, w_in, b_gate, sg_ln_g, sg_ln_b, sg_w, sg_b, w_a_out,
              cv_w, cv_b, cv_ln_g, cv_ln_b, w_b_out, q_norm_g, k_norm_g, w_c_out,
              w_out, ln2_g, w_up, ffn_conv_w, ffn_conv_b, w_down):
    bsz, t, d = x.shape
    split_at = np.cumsum([2 * SG_WIDTH, 2 * CV_WIDTH, 3 * SB_WIDTH]).tolist()
    for l in range(DEPTH):
        h = rms_norm(x, ln1_g[l])
        z = h @ w_in[l]
        z_a, z_b, z_c, z_g = jnp.split(z, split_at, axis=-1)

        a = jax.nn.gelu(z_a)
        u, v = jnp.split(a, 2, axis=-1)
        v = layer_norm(v, sg_ln_g[l], sg_ln_b[l])
        y_a = (u * chunked_spatial_gating(v, sg_w[l], sg_b[l])) @ w_a_out[l]

        p, g_lin = jnp.split(z_b, 2, axis=-1)
        c = p * jax.nn.sigmoid(g_lin)
        c = causal_depthwise_conv(c, cv_w[l], cv_b[l])
        c = jax.nn.silu(layer_norm(c, cv_ln_g[l], cv_ln_b[l]))
        y_b = c @ w_b_out[l]

        q, k, vv = jnp.split(z_c, 3, axis=-1)
        q = rms_norm(q.reshape(bsz, t, SB_HEADS, SB_HEAD_DIM), q_norm_g[l])
        k = rms_norm(k.reshape(bsz, t, SB_HEADS, SB_HEAD_DIM), k_norm_g[l])
        vv = vv.reshape(bsz, t, SB_HEADS, SB_HEAD_DIM)
        y_c = stick_breaking_attention(q, k, vv) @ w_c_out[l]

        gates = jax.nn.sigmoid(z_g.reshape(bsz, t, N_BRANCH, d) + b_gate[l].astype(z_g.dtype))
        merged = gates[:, :, 0] * y_a + gates[:, :, 1] * y_b + gates[:, :, 2] * y_c
        x = x + merged @ w_out[l]

        h2 = rms_norm(x, ln2_g[l])
        up = causal_depthwise_conv(h2 @ w_up[l], ffn_conv_w[l], ffn_conv_b[l])
        gate, val = jnp.split(up, 2, axis=-1)
        x = x + (jax.nn.silu(gate) * val) @ w_down[l]
    return x
```

```python
import numpy as np
import ml_dtypes
import concourse.bass as bass
import concourse.mybir as mybir
from concourse.bass_utils import run_bass_kernel_spmd

F32 = mybir.dt.float32
BF16 = mybir.dt.bfloat16
AF = mybir.ActivationFunctionType
ALU = mybir.AluOpType

FUSED = True

DEPTH = 4
T = 2048
D = 1024
TT = 512
NTT = 4
EPS = 1e-6
NSLOT = 6
SLOT_N = 1024

P_LN1G, P_LN2G, P_BG, P_CVB, P_CVLG, P_CVLB, P_CVW, P_QG, P_KG, P_FW, P_FB, NP = (
    0, 8, 16, 40, 44, 48, 52, 176, 177, 178, 310, 354)
C_ID, C_U, C_M2, C_M1, C_ONE, C_BD, NC_ = 0, 128, 256, 384, 512, 640, 768

FFN_TILES = [(0, 510), (510, 510), (1020, 510), (1530, 510), (2040, 8)]


class Unit:
    __slots__ = ("w", "r")

    def __init__(self):
        self.w = None
        self.r = {}


class KB:
    def __init__(self, nc):
        self.nc = nc
        self.eng = {"pe": nc.tensor, "act": nc.scalar, "dve": nc.vector, "pool": nc.gpsimd, "sp": nc.sync}
        self.semh = {}
        self.cnt = {}
        self.seen = {e: {} for e in self.eng}
        for e in self.eng:
            self.semh[e] = nc.alloc_semaphore(name="s_" + e)
            self.cnt[e] = 0
        self.pool_pending = {}

    def new_sem(self, name):
        self.semh[name] = self.nc.alloc_semaphore(name="s_" + name)
        self.cnt[name] = 0

    def _deps(self, reads, writes):
        need = {}
        for u in reads:
            if u.w is not None:
                s, v = u.w
                if v > need.get(s, 0):
                    need[s] = v
        for u in writes:
            if u.w is not None:
                s, v = u.w
                if v > need.get(s, 0):
                    need[s] = v
            for s, v in u.r.items():
                if v > need.get(s, 0):
                    need[s] = v
        return need

    def _wait(self, e, need):
        seen = self.seen[e]
        for s, v in need.items():
            if e == "pe" and s == "pe":
                continue
            if seen.get(s, 0) >= v:
                continue
            self.eng[e].wait_ge(self.semh[s], v)
            seen[s] = v

    def op(self, e, fn, reads=(), writes=()):
        need = self._deps(reads, writes)
        if e == "pool" and self.pool_pending:
            for s, v in self.pool_pending.items():
                if v > need.get(s, 0):
                    need[s] = v
            self.pool_pending = {}
        self._wait(e, need)
        ins = fn(self.eng[e])
        self.cnt[e] += 1
        ins.then_inc(self.semh[e], 1)
        c = self.cnt[e]
        for u in reads:
            u.r[e] = c
        for u in writes:
            u.w = (e, c)
            u.r = {}

    def dma(self, q, sem, out, in_, reads=(), writes=()):
        need = self._deps(reads, writes)
        self._wait(q, need)
        ins = self.eng[q].dma_start(out=out, in_=in_)
        self.cnt[sem] += 16
        ins.then_inc(self.semh[sem], 16)
        c = self.cnt[sem]
        for u in reads:
            u.r[sem] = c
        for u in writes:
            u.w = (sem, c)
            u.r = {}

    def barrier(self):
        comp = ("pe", "act", "dve")
        snap = {e: self.cnt[e] for e in comp + ("pool",)}
        for e in comp:
            need = {s: v for s, v in snap.items() if s != e and v > 0}
            self._wait(e, need)
        self.pool_pending = {e: snap[e] for e in comp if snap[e] > 0}

    def wait_all(self, e, sems):
        self._wait(e, {s: self.cnt[s] for s in sems if self.cnt[s] > 0})


def build_program(NL):
    nc = bass.Bass("TRN2", target_bir_lowering=False)
    dt = nc.dram_tensor
    xT_d = dt("xT", [D, T], F32, kind="ExternalInput").ap()
    cst_d = dt("consts", [128, NC_], F32, kind="ExternalInput").ap()
    win_d = dt("w_in_c", [NL, 52, 128, 1024], F32, kind="ExternalInput").ap()
    wv_d = dt("w_v", [NL, 4, 128, 1024], F32, kind="ExternalInput").ap()
    wabc_d = dt("w_abc", [NL, 3, 8, 128, 512], F32, kind="ExternalInput").ap()
    wout_d = dt("w_out_c", [NL, 8, 128, 1024], F32, kind="ExternalInput").ap()
    wup_d = dt("w_up_c", [NL, 44, 128, 1024], F32, kind="ExternalInput").ap()
    wdn_d = dt("w_dn", [NL, 2, 8, 128, 1408], F32, kind="ExternalInput").ap()
    pvec_d = dt("pvec", [NL, 128, NP], F32, kind="ExternalInput").ap()
    pbc_d = dt("pbc", [NL, 128, 1536], F32, kind="ExternalInput").ap()
    sgw_d = dt("sgw", [NL, 128, 1024], F32, kind="ExternalInput").ap()
    yT_d = dt("yT", [D, T], F32, kind="ExternalOutput").ap()

    kb = KB(nc)
    al = nc.alloc_sbuf_tensor

    xT = al("xT_sb", [128, 8, T], F32)
    hT = al("hT_sb", [128, 8, T + 2], BF16)
    scr = al("scr", [128, 16448], F32)
    ring = al("ring", [128, NSLOT, SLOT_N], BF16)
    tmp2 = al("tmp2", [128, 6656], F32)
    cstb = al("cstb", [128, NC_], BF16)
    onesf = al("onesf", [128, 128], F32)
    pvec = al("pvec_sb", [128, NP], F32)
    qgs = al("qgs", [128, 1], F32)
    negh = al("negh", [128, 1], F32)
    ps = nc.alloc_psum_tensor("ps", [128, 8, 512], F32)

    merged = scr[:, 0:8192].bitcast(BF16).rearrange("p (c t) -> p c t", c=8)
    S1 = scr[:, 8192:8192 + 4160].bitcast(BF16).rearrange("p (c t) -> p c t", c=4)
    S2 = scr[:, 12352:12352 + 4096].bitcast(BF16).rearrange("p (c t) -> p c t", c=4)
    act_ffn = scr[:, 0:11264].bitcast(BF16).rearrange("p (c t) -> p c t", c=11)

    def t2f(off, n):
        return tmp2[:, off:off + n]

    def t2b(off, n_words):
        return tmp2[:, off:off + n_words].bitcast(BF16)

    ident = cstb[:, C_ID:C_ID + 128]
    Umat = cstb[:, C_U:C_U + 128]
    M2b = cstb[:, C_M2:C_M2 + 128]
    M1b = cstb[:, C_M1:C_M1 + 128]
    onesb = cstb[:, C_ONE:C_ONE + 128]
    bd2 = cstb[:, C_BD:C_BD + 128]

    x_u = [[Unit() for _ in range(NTT)] for _ in range(8)]
    h_u = [Unit() for _ in range(NTT)]
    m_u = [Unit() for _ in range(NTT)]
    bank_u = [Unit() for _ in range(8)]
    slot_u = [Unit() for _ in range(NSLOT)]
    cst_u = Unit()
    onesf_u = Unit()
    pv_u = Unit()
    for s in range(NSLOT):
        kb.new_sem("w%d" % s)
    for nm in ("ld", "lc", "st", "pv", "pb", "sw"):
        kb.new_sem(nm)

    def bank(i):
        return ps[:, i, :]

    def h_units(lo, hi):
        lo = max(lo, 0)
        return [h_u[i] for i in range(lo // TT, min((hi - 1) // TT, NTT - 1) + 1)]

    tasks = []

    def task(entries, fn):
        tasks.append((entries, fn))

    def mm_group(out_ap, pairs, reads, writes):
        def fn(pe):
            n = len(pairs)
            ins = None
            for i, (l_, r_) in enumerate(pairs):
                ins = pe.matmul(out_ap, lhsT=l_, rhs=r_, start=(i == 0), stop=(i == n - 1))
            return ins
        kb.op("pe", fn, reads, writes)

    def A(e, fn, reads=(), writes=()):
        kb.op(e, fn, reads, writes)

    bank_rr = [0]

    def next_bank(lo=0, hi=8):
        b = lo + bank_rr[0] % (hi - lo)
        bank_rr[0] += 1
        return b

    def setup(slots, sus):
        kb.dma("pool", "lc", cstb[:, :], cst_d[:, :], writes=[cst_u])
        kb.dma("sp", "ld", onesf[:, :], cst_d[:, C_ONE:C_ONE + 128], writes=[onesf_u])
        for c in range(8):
            kb.dma("sp", "ld", xT[:, c, :], xT_d[c * 128:(c + 1) * 128, :], writes=x_u[c])
        fin = ("ld", kb.cnt["ld"])
        cst_u.w = ("lc", kb.cnt["lc"])
        onesf_u.w = fin
        for c in range(8):
            for u in x_u[c]:
                u.w = fin
        A("dve", lambda e: e.memset(hT[:, :, 0:2], 0.0), writes=[h_u[0]])
        A("dve", lambda e: e.memset(negh[:, :], -0.5), writes=[cst_u])

    task([], setup)

    def emit_rmsnorm(l, gcol):
        def fn_tt(tt):
            def fn(slots, sus):
                sqb = t2b(0, 2048).rearrange("p (c t) -> p c t", c=8)
                lnt = t2f(2048 + (tt % 2) * 1024, 512)
                rstd = t2f(2560 + (tt % 2) * 1024, 512)
                u_sq, u_ln, u_rs = usq[0], uu[tt % 2][0], uu[tt % 2][1]
                cs = slice(tt * TT, (tt + 1) * TT)
                xu = [x_u[c][tt] for c in range(8)]
                A("act", lambda e: e.activation(out=sqb, in_=xT[:, :, cs], func=AF.Square),
                  reads=xu, writes=[u_sq])
                b = next_bank()
                mm_group(bank(b), [(onesb, sqb[:, c, :]) for c in range(8)],
                         reads=[u_sq, cst_u], writes=[bank_u[b]])
                A("act", lambda e: e.activation(out=lnt, in_=bank(b), func=AF.Ln, scale=1.0 / D, bias=EPS),
                  reads=[bank_u[b]], writes=[u_ln])
                A("act", lambda e: e.activation(out=rstd, in_=lnt, func=AF.Exp, scale=-0.5),
                  reads=[u_ln], writes=[u_rs])
                for c in range(8):
                    A("dve", lambda e, c=c: e.scalar_tensor_tensor(
                        out=hT[:, c, 2 + tt * TT:2 + (tt + 1) * TT], in0=xT[:, c, cs],
                        scalar=pvec[:, gcol + c:gcol + c + 1], in1=rstd, op0=ALU.mult, op1=ALU.mult),
                      reads=[x_u[c][tt], u_rs, pv_u], writes=[h_u[tt]])
            return fn
        uu = [(Unit(), Unit()) for _ in range(2)]
        usq = [Unit(), Unit()]
        for tt in range(NTT):
            task([], fn_tt(tt))

    def emit_branch_out(l, br, src, first, sig_off, alias=()):
        src_u = emit_branch_out.src_units
        sig_units = [Unit(), Unit()]
        tmp_units = [Unit(), Unit()]
        for j in range(8):
            def fn(slots, sus, j=j):
                wb, wg = slots
                wb = wb[:, 0:512].rearrange("p (k c) -> p k c", k=4)
                wg = wg[:, 0:1024].rearrange("p (k c) -> p k c", k=8)
                su_b, su_g = sus
                for tt in range(NTT):
                    cs = slice(tt * TT, (tt + 1) * TT)
                    by = next_bank()
                    mm_group(bank(by), [(wb[:, k, :], src[:, k, cs]) for k in range(4)],
                             reads=[su_b] + [src_u[k][tt] for k in range(4)], writes=[bank_u[by]])
                    bg = next_bank()
                    mm_group(bank(bg), [(wg[:, k, :], hT[:, k, 2 + tt * TT:2 + (tt + 1) * TT]) for k in range(8)],
                             reads=[su_g, h_u[tt]], writes=[bank_u[bg]])
                    k2 = (j * NTT + tt) % 2
                    sig = t2f(sig_off + k2 * 512, 512)
                    A("act", lambda e: e.activation(out=sig, in_=bank(bg), func=AF.Sigmoid,
                                                    bias=pvec[:, P_BG + br * 8 + j:P_BG + br * 8 + j + 1]),
                      reads=[bank_u[bg], pv_u], writes=[sig_units[k2]] + list(alias))
                    if first:
                        A("dve", lambda e: e.tensor_tensor(out=merged[:, j, cs], in0=sig, in1=bank(by), op=ALU.mult),
                          reads=[sig_units[k2], bank_u[by]], writes=[m_u[tt]])
                    else:
                        A("dve", lambda e: e.tensor_tensor(out=sig, in0=sig, in1=bank(by), op=ALU.mult),
                          reads=[sig_units[k2], bank_u[by]], writes=[sig_units[k2]])
                        A("dve", lambda e: e.tensor_tensor(out=merged[:, j, cs], in0=merged[:, j, cs], in1=sig,
                                                           op=ALU.add),
                          reads=[sig_units[k2], m_u[tt]], writes=[m_u[tt]])
            task([(wabc_d[l, br, j], 512), (win_d[l, 28 + br * 8 + j], 1024)], fn)

    def emit_attention(l):
        oT = S2
        q_u = [[Unit() for _ in range(NTT)] for _ in range(4)]
        k_u = [[Unit() for _ in range(NTT)] for _ in range(4)]
        v_u = [[Unit() for _ in range(NTT)] for _ in range(4)]
        o_u = [[Unit() for _ in range(NTT)] for _ in range(4)]
        emit_branch_out.src_units = o_u
        sq_u = [Unit() for _ in range(3)]
        ln_u = [Unit() for _ in range(3)]
        zz_u = [Unit(), Unit()]
        rr_u = [Unit(), Unit()]
        bo_u = Unit()
        bo_units = [bo_u, bank_u[7]]
        e_u = [Unit(), Unit()]
        sp_u = [Unit(), Unit()]
        nl_u = [Unit(), Unit()]
        a_u = [Unit(), Unit()]
        s_u = [Unit(), Unit()]
        pc = [0]

        def zz(i):
            return ps[:, 2 * i:2 * i + 2, :]
        rr = ps[:, 4:6, :]

        def qscale():
            A("dve", lambda e: e.tensor_scalar(out=qgs[:, :], in0=pvec[:, P_QG:P_QG + 1], scalar1=0.125, scalar2=None,
                                               op0=ALU.mult),
              reads=[pv_u], writes=[cst_u])
        task([], lambda slots, sus: qscale())

        for hp in range(4):
            for which, chunk0 in (("q", 16), ("k", 20)):
                def fn(slots, sus, which=which, hp=hp):
                    w = slots[0][:, 0:1024].rearrange("p (k c) -> p k c", k=8)
                    su = sus[0]
                    gap = qgs[:, :] if which == "q" else pvec[:, P_KG:P_KG + 1]
                    dst = merged[:, hp, :] if which == "q" else merged[:, 4 + hp, :]
                    du = q_u[hp] if which == "q" else k_u[hp]
                    for tt in range(NTT):
                        cs = slice(tt * TT, (tt + 1) * TT)
                        r3 = pc[0] % 3
                        pc[0] += 1
                        sq_t = t2b(4096 + r3 * 256, 256)
                        ln_t = t2f(4096 + 768 + r3 * 512, 512)
                        b = next_bank()
                        mm_group(bank(b), [(w[:, k, :], hT[:, k, 2 + tt * TT:2 + (tt + 1) * TT]) for k in range(8)],
                                 reads=[su, h_u[tt]], writes=[bank_u[b]])
                        A("act", lambda e: e.activation(out=sq_t, in_=bank(b), func=AF.Square),
                          reads=[bank_u[b]], writes=[sq_u[r3]])
                        b2 = next_bank()
                        mm_group(bank(b2), [(bd2, sq_t)], reads=[sq_u[r3], cst_u], writes=[bank_u[b2]])
                        A("act", lambda e: e.activation(out=ln_t, in_=bank(b2), func=AF.Ln, scale=1.0 / 64, bias=EPS),
                          reads=[bank_u[b2]], writes=[ln_u[r3]])
                        A("act", lambda e: e.activation(out=ln_t, in_=ln_t, func=AF.Exp, scale=-0.5),
                          reads=[ln_u[r3]], writes=[ln_u[r3]])
                        A("dve", lambda e: e.scalar_tensor_tensor(out=dst[:, cs], in0=bank(b), scalar=gap, in1=ln_t,
                                                                  op0=ALU.mult, op1=ALU.mult),
                          reads=[bank_u[b], ln_u[r3], cst_u, pv_u], writes=[du[tt]])
                task([(win_d[l, chunk0 + hp], 1024)], fn)

            def fnv(slots, sus, hp=hp):
                w = slots[0][:, 0:1024].rearrange("p (k c) -> p k c", k=8)
                su = sus[0]
                vvh = S1[:, hp, 0:T].rearrange("p (c d) -> p c d", c=16)
                for tt in range(NTT):
                    b = next_bank()
                    for ci in range(4):
                        c = tt * 4 + ci
                        mm_group(ps[:, b, ci * 128:(ci + 1) * 128],
                                 [(hT[:, k, 2 + c * 128:2 + (c + 1) * 128], w[:, k, :]) for k in range(8)],
                                 reads=[su, h_u[tt]], writes=[bank_u[b]])
                    A("act", lambda e: e.activation(out=vvh[:, tt * 4:(tt + 1) * 4, :],
                                                    in_=bank(b).rearrange("p (c d) -> p c d", c=4), func=AF.Copy),
                      reads=[bank_u[b]], writes=[v_u[hp][tt]])
            task([(win_d[l, 24 + hp], 1024)], fnv)
        task([], lambda slots, sus: kb.barrier())

        if True:
            def fna(slots, sus):
                steps = []
                for hp_ in range(4):
                    for g in range(NTT):
                        top = 4 * g + 3
                        for kb_ in range(top, -1, -1):
                            steps.append((hp_, g, kb_))
                n = len(steps)

                def geom(i):
                    hp, g, kb_ = steps[i]
                    j = kb_ - 4 * g
                    c0 = 128 * j if j > 0 else 0
                    si = i % 2
                    return dict(
                        hp=hp, grp=hp * NTT + g,
                        qT=merged[:, hp, :], kT=merged[:, 4 + hp, :],
                        vv=S1[:, hp, 0:T].rearrange("p (c d) -> p c d", c=16),
                        g=g, kb=kb_, c0=c0, diag=(j >= 0), first=(kb_ == 4 * g + 3), si=si, z=zz(si),
                        eb=t2f(si * 2560, 1024).rearrange("p (h t) -> p h t", h=2),
                        spb=t2b(si * 2560 + 1024, 512).rearrange("p (h t) -> p h t", h=2),
                        nlb=t2b(si * 2560 + 1536, 512).rearrange("p (h t) -> p h t", h=2),
                        ab=t2b(si * 2560 + 2048, 512).rearrange("p (h t) -> p h t", h=2),
                        qcols=slice(g * TT + c0, (g + 1) * TT), kcols=slice(kb_ * 128, (kb_ + 1) * 128),
                        ktt=kb_ // 4)
                Sb = t2b(5120, 512).rearrange("p (h t) -> p h t", h=2)
                m1 = M1b.unsqueeze(1).broadcast_to([128, 2, 128])

                def stage_a(i):
                    G = geom(i)
                    c0, si, z, eb, spb, nlb = G["c0"], G["si"], G["z"], G["eb"], G["spb"], G["nlb"]

                    def qk(pe):
                        ins = None
                        for h in range(2):
                            hs = slice(h * 64, (h + 1) * 64)
                            ins = pe.matmul(z[:, h, c0:512], lhsT=G["kT"][hs, G["kcols"]], rhs=G["qT"][hs, G["qcols"]],
                                            start=True, stop=True)
                        return ins
                    A("pe", qk, reads=[k_u[G["hp"]][G["ktt"]], q_u[G["hp"]][G["g"]]], writes=[zz_u[si]])
                    A("act", lambda e: e.activation(out=eb[:, :, c0:512], in_=z[:, :, c0:512], func=AF.Exp, scale=-1.0),
                      reads=[zz_u[si]], writes=[e_u[si]])
                    A("act", lambda e: e.activation(out=spb[:, :, c0:512], in_=eb[:, :, c0:512], func=AF.Ln,
                                                    scale=1.0, bias=1.0),
                      reads=[e_u[si]], writes=[sp_u[si]])

                def stage_a2(i):
                    G = geom(i)
                    c0, si, z, eb, spb, nlb = G["c0"], G["si"], G["z"], G["eb"], G["spb"], G["nlb"]
                    A("dve", lambda e: e.tensor_tensor(out=nlb[:, :, c0:512], in0=spb[:, :, c0:512],
                                                       in1=z[:, :, c0:512], op=ALU.add),
                      reads=[sp_u[si], zz_u[si]], writes=[nl_u[si]])
                    if G["diag"]:
                        A("dve", lambda e: e.tensor_tensor(out=nlb[:, :, c0:c0 + 128], in0=nlb[:, :, c0:c0 + 128],
                                                           in1=m1, op=ALU.mult),
                          reads=[nl_u[si], cst_u], writes=[nl_u[si]])

                def stage_b(i):
                    G = geom(i)
                    c0, si, eb, spb, nlb, ab = G["c0"], G["si"], G["eb"], G["spb"], G["nlb"], G["ab"]
                    first = G["first"]
                    for h in range(2):
                        def rmm(pe, h=h):
                            pe.matmul(rr[:, h, c0:512], lhsT=Umat, rhs=nlb[:, h, c0:512], start=True, stop=False)
                            if not first:
                                pe.matmul(rr[:, h, c0:512], lhsT=onesb, rhs=Sb[:, h, c0:512], start=False, stop=False)
                            return pe.matmul(rr[:, h, c0:512], lhsT=ident, rhs=spb[:, h, c0:512], start=False, stop=True)
                        A("pe", rmm, reads=[nl_u[si], sp_u[si], s_u[h], cst_u], writes=[rr_u[h]])
                    for h in range(2):
                        if G["kb"] > 0:
                            if first:
                                A("dve", lambda e, h=h: e.memset(Sb[:, h, :], 0.0), writes=[s_u[h]])
                            A("dve", lambda e, h=h: e.tensor_tensor(out=Sb[:, h, c0:512], in0=Sb[:, h, c0:512],
                                                                    in1=nlb[:, h, c0:512], op=ALU.add),
                              reads=[nl_u[si], s_u[h]], writes=[s_u[h]])

                def stage_b2(i):
                    G = geom(i)
                    c0, si, eb, spb, nlb, ab = G["c0"], G["si"], G["eb"], G["spb"], G["nlb"], G["ab"]
                    A("act", lambda e: e.activation(out=ab[:, :, c0:512], in_=rr[:, :, c0:512], func=AF.Exp, scale=-1.0),
                      reads=rr_u, writes=[a_u[si]])
                    if G["diag"]:
                        A("dve", lambda e: e.tensor_tensor(out=ab[:, :, c0:c0 + 128], in0=ab[:, :, c0:c0 + 128],
                                                           in1=m1, op=ALU.mult),
                          reads=[a_u[si], cst_u], writes=[a_u[si]])

                def stage_c(i):
                    G = geom(i)
                    c0, si, ab, g, kb_, hp, vv = G["c0"], G["si"], G["ab"], G["g"], G["kb"], G["hp"], G["vv"]
                    bsel = G["grp"] % 2
                    bi = 6 + bsel

                    def av(pe):
                        ins = None
                        for h in range(2):
                            hs = slice(h * 64, (h + 1) * 64)
                            ins = pe.matmul(ps[hs, bi, c0:512], lhsT=vv[:, kb_, hs], rhs=ab[:, h, c0:512],
                                            start=G["first"], stop=(kb_ == 0), skip_group_check=True)
                        return ins
                    A("pe", av, reads=[a_u[si], v_u[hp][G["ktt"]]], writes=[bo_units[bsel]])
                    if kb_ == 0:
                        A("act", lambda e: e.activation(out=oT[:, hp, g * TT:(g + 1) * TT], in_=ps[:, bi, :],
                                                        func=AF.Copy),
                          reads=[bo_units[bsel]], writes=[o_u[hp][g]])

                for i in range(n + 2):
                    if i < n:
                        stage_a(i)
                    if 0 <= i - 1 < n:
                        stage_b(i - 1)
                    if i < n:
                        stage_a2(i)
                    if 0 <= i - 1 < n:
                        stage_b2(i - 1)
                    if 0 <= i - 2 < n:
                        stage_c(i - 2)
            task([], fna)

    def emit_sgu(l):
        uT = S1
        vtok = S2.rearrange("p c t -> p (c t)").rearrange("p (c f) -> p c f", c=16)
        u_u = [[Unit() for _ in range(NTT)] for _ in range(4)]
        v_u = [Unit() for _ in range(16)]
        emit_branch_out.src_units = u_u
        pbc = t2f(0, 1536)
        sgw_raw = t2f(1536, 1024).rearrange("p (g t) -> p g t", g=8)
        WmT = t2b(2560, 512).rearrange("p (g t) -> p g t", g=8)
        pb_u, wm_u = Unit(), Unit()
        emit_sgu.alias = [wm_u]
        vg_u = [Unit() for _ in range(4)]
        st_u = [Unit() for _ in range(4)]
        tm_u = [Unit(), Unit()]

        def load_params(slots, sus):
            kb.wait_all("sp", ["pe", "act", "dve", "pool"])
            kb.dma("sp", "pb", pbc, pbc_d[l], writes=[pb_u])
            kb.dma("sp", "sw", t2f(1536, 1024), sgw_d[l], writes=[wm_u])
            A("dve", lambda e: e.tensor_tensor(out=WmT, in0=sgw_raw,
                                               in1=M2b.unsqueeze(1).broadcast_to([128, 8, 128]), op=ALU.mult),
              reads=[wm_u, cst_u], writes=[wm_u])
        task([], load_params)

        for j in range(4):
            def fn(slots, sus, j=j):
                w = slots[0][:, 0:1024].rearrange("p (k c) -> p k c", k=8)
                su = sus[0]
                for tt in range(NTT):
                    b = next_bank()
                    mm_group(bank(b), [(w[:, k, :], hT[:, k, 2 + tt * TT:2 + (tt + 1) * TT]) for k in range(8)],
                             reads=[su, h_u[tt]], writes=[bank_u[b]])
                    A("act", lambda e: e.activation(out=uT[:, j, tt * TT:(tt + 1) * TT], in_=bank(b),
                                                    func=AF.Gelu_apprx_tanh),
                      reads=[bank_u[b]], writes=[u_u[j][tt]])
            task([(win_d[l, j], 1024)], fn)

        def fnv(slots, sus):
            ws = [s_[:, 0:1024].rearrange("p (k f) -> p k f", k=2) for s_ in slots]
            for c in range(16):
                k2 = c % 4
                b = next_bank()
                mm_group(bank(b), [(hT[:, k, 2 + c * 128:2 + (c + 1) * 128], ws[k // 2][:, k % 2, :]) for k in range(8)],
                         reads=list(sus) + [h_u[c // 4]], writes=[bank_u[b]])
                vg = t2f((3072 + k2 * 512) if k2 < 2 else (5184 + (k2 - 2) * 512), 512)
                bst = t2f(4096 + k2 * 16, 6)
                mv = t2f(4096 + k2 * 16 + 6, 2)
                rs = t2f(4096 + k2 * 16 + 8, 1)
                A("act", lambda e: e.activation(out=vg, in_=bank(b), func=AF.Gelu_apprx_tanh),
                  reads=[bank_u[b]], writes=[vg_u[k2]])
                A("dve", lambda e: e.bn_stats(out=bst, in_=vg), reads=[vg_u[k2]], writes=[st_u[k2]])
                A("dve", lambda e: e.bn_aggr(out=mv, in_=bst), reads=[st_u[k2]], writes=[st_u[k2]])
                A("dve", lambda e: e.tensor_scalar(out=rs, in0=mv[:, 1:2], scalar1=EPS, scalar2=None, op0=ALU.add),
                  reads=[st_u[k2]], writes=[st_u[k2]])
                A("pool", lambda e: e.tensor_tensor(out=rs, in0=rs, in1=negh[:, :], op=ALU.pow),
                  reads=[st_u[k2], cst_u], writes=[st_u[k2]])
                A("dve", lambda e: e.tensor_scalar(out=vg, in0=vg, scalar1=mv[:, 0:1], scalar2=rs,
                                                   op0=ALU.subtract, op1=ALU.mult),
                  reads=[vg_u[k2], st_u[k2]], writes=[vg_u[k2]])
                A("dve", lambda e: e.tensor_tensor(out=vg, in0=vg, in1=pbc[:, 0:512], op=ALU.mult),
                  reads=[vg_u[k2], pb_u], writes=[vg_u[k2]])
                A("dve", lambda e: e.tensor_tensor(out=vtok[:, c, :], in0=vg, in1=pbc[:, 512:1024], op=ALU.add),
                  reads=[vg_u[k2], pb_u], writes=[v_u[c]])
        task([(wv_d[l, e_], 1024) for e_ in range(4)], fnv)

        def fng(slots, sus):
            sgb = pbc[:, 1024:1536].rearrange("p (g t) -> p g t", g=4)
            for tt in range(NTT):
                for gp in range(4):
                    b = next_bank()

                    def sg(pe):
                        ins = None
                        for ci in range(4):
                            c = tt * 4 + ci
                            for gh in range(2):
                                g = gp * 2 + gh
                                ins = pe.matmul(ps[gh * 64:(gh + 1) * 64, b, ci * 128:(ci + 1) * 128],
                                                lhsT=vtok[:, c, g * 64:(g + 1) * 64], rhs=WmT[:, g, :],
                                                start=True, stop=True)
                        return ins
                    A("pe", sg, reads=[v_u[tt * 4 + ci] for ci in range(4)] + [wm_u], writes=[bank_u[b]])
                    k2 = (tt * 4 + gp) % 2
                    tm = t2f(4160 + k2 * 512, 512)
                    A("dve", lambda e: e.tensor_tensor(
                        out=tm.rearrange("p (c t) -> p c t", c=4), in0=bank(b).rearrange("p (c t) -> p c t", c=4),
                        in1=sgb[:, gp, :].unsqueeze(1).broadcast_to([128, 4, 128]), op=ALU.add),
                      reads=[bank_u[b], pb_u], writes=[tm_u[k2]])
                    cs = slice(tt * TT, (tt + 1) * TT)
                    A("dve", lambda e: e.tensor_tensor(out=uT[:, gp, cs], in0=uT[:, gp, cs], in1=tm, op=ALU.mult),
                      reads=[tm_u[k2], u_u[gp][tt]], writes=[u_u[gp][tt]])
        task([], fng)

    def emit_conv(l):
        c0T = S1
        cT = S2
        c0_u = [[Unit() for _ in range(NTT)] for _ in range(4)]
        c_u = [[Unit() for _ in range(NTT)] for _ in range(4)]
        emit_branch_out.src_units = c_u
        sg_u = [Unit(), Unit()]

        def fpad(slots, sus):
            A("dve", lambda e: e.memset(c0T[:, :, 0:30], 0.0), writes=[c0_u[j][0] for j in range(4)])
        task([], fpad)
        for j in range(4):
            def fn(slots, sus, j=j):
                wp = slots[0][:, 0:1024].rearrange("p (k c) -> p k c", k=8)
                wg = slots[1][:, 0:1024].rearrange("p (k c) -> p k c", k=8)
                sup, sug = sus
                for tt in range(NTT):
                    hcs = slice(2 + tt * TT, 2 + (tt + 1) * TT)
                    bp = next_bank()
                    mm_group(bank(bp), [(wp[:, k, :], hT[:, k, hcs]) for k in range(8)],
                             reads=[sup, h_u[tt]], writes=[bank_u[bp]])
                    bg = next_bank()
                    mm_group(bank(bg), [(wg[:, k, :], hT[:, k, hcs]) for k in range(8)],
                             reads=[sug, h_u[tt]], writes=[bank_u[bg]])
                    k2 = (j * NTT + tt) % 2
                    sg = t2f(5056 + k2 * 512, 512)
                    A("act", lambda e: e.activation(out=sg, in_=bank(bg), func=AF.Sigmoid),
                      reads=[bank_u[bg]], writes=[sg_u[k2]])
                    A("dve", lambda e: e.tensor_tensor(out=c0T[:, j, 30 + tt * TT:30 + (tt + 1) * TT], in0=bank(bp),
                                                       in1=sg, op=ALU.mult),
                      reads=[bank_u[bp], sg_u[k2]], writes=[c0_u[j][tt]])
            task([(win_d[l, 8 + j], 1024), (win_d[l, 12 + j], 1024)], fn)
        task([], lambda slots, sus: kb.barrier())

        diag = t2b(0, 1984).rearrange("p (k c) -> p k c", k=31)
        c1 = t2f(1984, 2048).rearrange("p (j t) -> p j t", j=4)
        sqc = t2b(4032, 1024).rearrange("p (j t) -> p j t", j=4)
        mu = t2f(5056, 512)
        var = t2f(5568, 512)
        msq = t2f(6080, 512)
        dg_u, c1_u, sqc_u, mu_u, var_u, msq_u = [Unit(), Unit()], [Unit() for _ in range(4)], Unit(), Unit(), Unit(), Unit()
        emit_conv.alias = [sqc_u]

        def build_diag(j, hf, k0, k1):
            A("dve", lambda e: e.tensor_tensor(
                out=diag[:, k0:k1, :], in0=ident.unsqueeze(1).broadcast_to([128, k1 - k0, 128]),
                in1=pvec[:, P_CVW + j * 31 + k0:P_CVW + j * 31 + k1].unsqueeze(2).broadcast_to([128, k1 - k0, 128]),
                op=ALU.mult),
              reads=[cst_u, pv_u], writes=[dg_u[hf]])

        def fnc(slots, sus):
            for tt in range(NTT):
                for j in range(4):
                    b = next_bank(0, 4)
                    rd = [c0_u[j][tt]] + ([c0_u[j][tt - 1]] if tt > 0 else [])
                    for hf, (k0, k1) in enumerate(((0, 16), (16, 31))):
                        if not (tt > 0 and j == 0):
                            build_diag(j, hf, k0, k1)

                        def cmm(pe, k0=k0, k1=k1):
                            ins = None
                            for k in range(k0, k1):
                                ins = pe.matmul(bank(b), lhsT=diag[:, k, :], rhs=c0T[:, j, tt * TT + k:tt * TT + k + TT],
                                                start=(k == 0), stop=(k == 30))
                            return ins
                        A("pe", cmm, reads=rd + [dg_u[hf]], writes=[bank_u[b]])
                    cvb = pvec[:, P_CVB + j:P_CVB + j + 1]
                    A("act", lambda e: e.activation(out=c1[:, j, :], in_=bank(b), func=AF.Identity, bias=cvb),
                      reads=[bank_u[b], pv_u], writes=[c1_u[j]])
                    A("act", lambda e: e.activation(out=sqc[:, j, :], in_=bank(b), func=AF.Square, bias=cvb),
                      reads=[bank_u[b], pv_u], writes=[sqc_u])
                bm, bv = 4, 5
                mm_group(bank(bm), [(onesf[:, :], c1[:, j, :]) for j in range(4)],
                         reads=c1_u + [onesf_u], writes=[bank_u[bm]])
                mm_group(bank(bv), [(onesb, sqc[:, j, :]) for j in range(4)],
                         reads=[sqc_u, cst_u], writes=[bank_u[bv]])
                if tt + 1 < NTT:
                    build_diag(0, 0, 0, 16)
                    build_diag(0, 1, 16, 31)
                A("dve", lambda e: e.tensor_scalar(out=mu, in0=bank(bm), scalar1=1.0 / 512, scalar2=None, op0=ALU.mult),
                  reads=[bank_u[bm]], writes=[mu_u])
                A("dve", lambda e: e.tensor_tensor(out=msq, in0=mu, in1=mu, op=ALU.mult),
                  reads=[mu_u], writes=[msq_u])
                A("dve", lambda e: e.scalar_tensor_tensor(out=var, in0=bank(bv), scalar=1.0 / 512, in1=msq,
                                                          op0=ALU.mult, op1=ALU.subtract),
                  reads=[bank_u[bv], msq_u], writes=[var_u])
                A("act", lambda e: e.activation(out=var, in_=var, func=AF.Ln, scale=1.0, bias=EPS),
                  reads=[var_u], writes=[var_u])
                A("act", lambda e: e.activation(out=var, in_=var, func=AF.Exp, scale=-0.5),
                  reads=[var_u], writes=[var_u])
                for j in range(4):
                    A("dve", lambda e, j=j: e.tensor_tensor(out=c1[:, j, :], in0=c1[:, j, :], in1=mu, op=ALU.subtract),
                      reads=[c1_u[j], mu_u], writes=[c1_u[j]])
                    A("dve", lambda e, j=j: e.tensor_tensor(out=c1[:, j, :], in0=c1[:, j, :], in1=var, op=ALU.mult),
                      reads=[c1_u[j], var_u], writes=[c1_u[j]])
                    A("act", lambda e, j=j: e.activation(
                        out=cT[:, j, tt * TT:(tt + 1) * TT], in_=c1[:, j, :], func=AF.Silu,
                        scale=pvec[:, P_CVLG + j:P_CVLG + j + 1], bias=pvec[:, P_CVLB + j:P_CVLB + j + 1]),
                      reads=[c1_u[j], pv_u], writes=[c_u[j][tt]])
        task([], fnc)

    def emit_outproj(l):
        for j in range(8):
            def fn(slots, sus, j=j):
                w = slots[0][:, 0:1024].rearrange("p (k c) -> p k c", k=8)
                su = sus[0]
                for tt in range(NTT):
                    cs = slice(tt * TT, (tt + 1) * TT)
                    b = next_bank()
                    mm_group(bank(b), [(w[:, k, :], merged[:, k, cs]) for k in range(8)],
                             reads=[su, m_u[tt]], writes=[bank_u[b]])
                    A("dve", lambda e: e.tensor_tensor(out=xT[:, j, cs], in0=xT[:, j, cs], in1=bank(b), op=ALU.add),
                      reads=[bank_u[b], x_u[j][tt]], writes=[x_u[j][tt]])
            task([(wout_d[l, j], 1024)], fn)

    def emit_ffn(l):
        a_u = [[Unit() for _ in range(NTT)] for _ in range(11)]
        tg_u = [Unit(), Unit()]
        tv_u = [Unit(), Unit()]

        def a_units(jj, lo, hi):
            return [a_u[jj][i] for i in range(lo // TT, (hi - 1) // TT + 1)]
        for grp in range(2):
            for jj in range(11):
                j = grp * 11 + jj

                def fn(slots, sus, j=j, jj=jj):
                    wg = slots[0][:, 0:1024].rearrange("p (k c) -> p k c", k=8)
                    wv = slots[1][:, 0:1024].rearrange("p (k c) -> p k c", k=8)
                    sug, suv = sus
                    for ti, (st, ln) in enumerate(FFN_TILES):
                        nn = ln + 2
                        k2 = ti % 2
                        hu = h_units(st - 2, st + ln)
                        bg = next_bank()
                        mm_group(ps[:, bg, 0:nn], [(wg[:, k, :], hT[:, k, st:st + nn]) for k in range(8)],
                                 reads=[sug] + hu, writes=[bank_u[bg]])
                        bv = next_bank()
                        mm_group(ps[:, bv, 0:nn], [(wv[:, k, :], hT[:, k, st:st + nn]) for k in range(8)],
                                 reads=[suv] + hu, writes=[bank_u[bv]])
                        tg = t2f(4096 + k2 * 1024, 512)[:, 0:ln]
                        tv = t2f(4096 + k2 * 1024 + 512, 512)[:, 0:ln]
                        for (bk, tbuf, tu, ch) in ((bg, tg, tg_u[k2], j), (bv, tv, tv_u[k2], 22 + j)):
                            def wcol(k, ch=ch):
                                return pvec[:, P_FW + k * 44 + ch:P_FW + k * 44 + ch + 1]
                            fb = pvec[:, P_FB + ch:P_FB + ch + 1]
                            A("act", lambda e, bk=bk, tbuf=tbuf, wcol=wcol, fb=fb: e.activation(
                                out=tbuf, in_=ps[:, bk, 2:nn], func=AF.Identity, scale=wcol(2), bias=fb),
                              reads=[bank_u[bk], pv_u], writes=[tu])
                            A("dve", lambda e, bk=bk, tbuf=tbuf, wcol=wcol: e.scalar_tensor_tensor(
                                out=tbuf, in0=ps[:, bk, 1:nn - 1], scalar=wcol(1), in1=tbuf, op0=ALU.mult, op1=ALU.add),
                              reads=[bank_u[bk], pv_u, tu], writes=[tu])
                            A("dve", lambda e, bk=bk, tbuf=tbuf, wcol=wcol: e.scalar_tensor_tensor(
                                out=tbuf, in0=ps[:, bk, 0:nn - 2], scalar=wcol(0), in1=tbuf, op0=ALU.mult, op1=ALU.add),
                              reads=[bank_u[bk], pv_u, tu], writes=[tu])
                        A("act", lambda e: e.activation(out=tg, in_=tg, func=AF.Silu),
                          reads=[tg_u[k2]], writes=[tg_u[k2]])
                        A("dve", lambda e: e.tensor_tensor(out=act_ffn[:, jj, st:st + ln], in0=tg, in1=tv, op=ALU.mult),
                          reads=[tg_u[k2], tv_u[k2]], writes=a_units(jj, st, st + ln))
                task([(wup_d[l, j], 1024), (wup_d[l, 22 + j], 1024)], fn)
            for dj in range(8):
                def fnd(slots, sus, dj=dj):
                    w0 = slots[0][:, 0:768].rearrange("p (k c) -> p k c", k=6)
                    w1 = slots[1][:, 0:640].rearrange("p (k c) -> p k c", k=5)
                    su0, su1 = sus
                    for tt in range(NTT):
                        cs = slice(tt * TT, (tt + 1) * TT)
                        b = next_bank()
                        pairs = [(w0[:, k, :], act_ffn[:, k, cs]) for k in range(6)] + \
                                [(w1[:, k, :], act_ffn[:, 6 + k, cs]) for k in range(5)]
                        mm_group(bank(b), pairs, reads=[su0, su1] + [a_u[k][tt] for k in range(11)],
                                 writes=[bank_u[b]])
                        A("dve", lambda e: e.tensor_tensor(out=xT[:, dj, cs], in0=xT[:, dj, cs], in1=bank(b), op=ALU.add),
                          reads=[bank_u[b], x_u[dj][tt]], writes=[x_u[dj][tt]])
                task([(wdn_d[l, grp, dj][:, 0:768], 768), (wdn_d[l, grp, dj][:, 768:1408], 640)], fnd)

    def emit_barrier():
        task([], lambda slots, sus: kb.barrier())

    for l in range(NL):
        task([], lambda slots, sus, l=l: kb.dma("sp", "pv", pvec[:, :], pvec_d[l], writes=[pv_u]))
        emit_rmsnorm(l, P_LN1G)
        emit_attention(l)
        emit_barrier()
        emit_branch_out(l, 2, S2, True, 0)
        emit_barrier()
        emit_sgu(l)
        emit_branch_out(l, 0, S1[:, :, 0:T], False, 1536, alias=emit_sgu.alias)
        emit_barrier()
        emit_conv(l)
        emit_branch_out(l, 1, S2, False, 4032, alias=emit_conv.alias)
        emit_outproj(l)
        emit_barrier()
        emit_rmsnorm(l, P_LN2G)
        emit_ffn(l)
        emit_barrier()

    def store(slots, sus):
        for c in range(8):
            kb.dma("sp", "st", yT_d[c * 128:(c + 1) * 128, :], xT[:, c, :], reads=x_u[c])
        kb.wait_all("sp", ["st"])
    task([], store)

    ent = []
    first_ent = []
    for (es, fn) in tasks:
        first_ent.append(len(ent))
        ent.extend(es)
    next_dma = [0]

    def issue(m):
        s = m % NSLOT
        ap, n = ent[m]
        kb.dma("pool", "w%d" % s, ring[:, s, 0:n], ap, writes=[slot_u[s]])

    LOOK = NSLOT
    for ti, (es, fn) in enumerate(tasks):
        f0 = first_ent[ti]
        last = f0 + len(es) - 1
        assert len(es) <= NSLOT
        while next_dma[0] < len(ent) and next_dma[0] - NSLOT < f0 and next_dma[0] <= max(last, f0 + LOOK - 1):
            issue(next_dma[0])
            next_dma[0] += 1
        assert next_dma[0] > last
        fn([ring[:, (f0 + i) % NSLOT, :] for i in range(len(es))],
           [slot_u[(f0 + i) % NSLOT] for i in range(len(es))])
    return nc


def _consts():
    c = np.zeros((128, NC_), np.float32)
    i = np.arange(128)
    c[:, C_ID:C_ID + 128] = np.eye(128)
    c[:, C_U:C_U + 128] = (i[:, None] > i[None, :])
    c[:, C_M2:C_M2 + 128] = (i[:, None] <= i[None, :])
    c[:, C_M1:C_M1 + 128] = (i[:, None] < i[None, :])
    c[:, C_ONE:C_ONE + 128] = 1.0
    c[:, C_BD:C_BD + 128] = ((i[:, None] // 64) == (i[None, :] // 64))
    return c


def _chunked(w, kc):
    ncol = w.shape[1]
    return np.ascontiguousarray(
        w.reshape(kc, 128, ncol // 128, 128).transpose(2, 1, 0, 3)).reshape(ncol // 128, 128, kc * 128)


def _prep_layer(inp, l):
    f = lambda a: np.asarray(a, dtype=np.float32)
    w_in = f(inp["w_in"][l])
    out = {}
    out["w_in_c"] = _chunked(w_in, 8)
    wv = w_in[:, 512:1024]
    out["w_v"] = np.ascontiguousarray(wv.reshape(4, 2, 128, 512).transpose(0, 2, 1, 3)).reshape(4, 128, 1024)
    out["w_abc"] = np.stack([_chunked(f(inp[k][l]), 4) for k in ("w_a_out", "w_b_out", "w_c_out")])
    out["w_out_c"] = _chunked(f(inp["w_out"][l]), 8)
    out["w_up_c"] = _chunked(f(inp["w_up"][l]), 8)
    wd = f(inp["w_down"][l])
    out["w_dn"] = np.ascontiguousarray(wd.reshape(2, 11, 128, 8, 128).transpose(0, 3, 2, 1, 4)).reshape(2, 8, 128, 1408)
    pv = np.zeros((128, NP), np.float32)
    pv[:, P_LN1G:P_LN1G + 8] = f(inp["ln1_g"][l]).reshape(8, 128).T
    pv[:, P_LN2G:P_LN2G + 8] = f(inp["ln2_g"][l]).reshape(8, 128).T
    pv[:, P_BG:P_BG + 24] = f(inp["b_gate"][l]).reshape(3, 8, 128).transpose(2, 0, 1).reshape(128, 24)
    pv[:, P_CVB:P_CVB + 4] = f(inp["cv_b"][l]).reshape(4, 128).T
    pv[:, P_CVLG:P_CVLG + 4] = f(inp["cv_ln_g"][l]).reshape(4, 128).T
    pv[:, P_CVLB:P_CVLB + 4] = f(inp["cv_ln_b"][l]).reshape(4, 128).T
    pv[:, P_CVW:P_CVW + 124] = f(inp["cv_w"][l]).reshape(31, 4, 128).transpose(2, 1, 0).reshape(128, 124)
    pv[:, P_QG] = np.tile(f(inp["q_norm_g"][l]), 2)
    pv[:, P_KG] = np.tile(f(inp["k_norm_g"][l]), 2)
    pv[:, P_FW:P_FW + 132] = f(inp["ffn_conv_w"][l]).reshape(3, 44, 128).transpose(2, 0, 1).reshape(128, 132)
    pv[:, P_FB:P_FB + 44] = f(inp["ffn_conv_b"][l]).reshape(44, 128).T
    out["pvec"] = pv
    pb = np.zeros((128, 1536), np.float32)
    pb[:, 0:512] = f(inp["sg_ln_g"][l])[None, :]
    pb[:, 512:1024] = f(inp["sg_ln_b"][l])[None, :]
    sgb = f(inp["sg_b"][l])
    pb[:, 1024:1536] = np.repeat(sgb.reshape(4, 2, 128), 64, axis=1).transpose(1, 0, 2).reshape(128, 512)
    out["pbc"] = pb
    out["sgw"] = np.ascontiguousarray(f(inp["sg_w"][l]).transpose(2, 0, 1)).reshape(128, 1024)
    return out


_PROG = {}


def _get_prog(nl):
    if nl not in _PROG:
        _PROG[nl] = build_program(nl)
    return _PROG[nl]


def kernel(**inputs):
    x = np.asarray(inputs["x"], dtype=np.float32)
    B = x.shape[0]
    layers = [_prep_layer(inputs, l) for l in range(DEPTH)]
    cst = _consts()
    xT = [np.ascontiguousarray(x[b].T) for b in range(B)]
    groups = [list(range(DEPTH))] if FUSED else [[l] for l in range(DEPTH)]
    for grp in groups:
        nc = _get_prog(len(grp))
        wmaps = {k: np.stack([layers[l][k] for l in grp]) for k in layers[0]}
        in_maps = []
        for b in range(B):
            m = {"xT": xT[b], "consts": cst}
            m.update(wmaps)
            in_maps.append(m)
        res = run_bass_kernel_spmd(nc, in_maps, core_ids=list(range(B)))
        xT = [np.asarray(res.results[b]["yT"], dtype=np.float32) for b in range(B)]
    return np.stack([xT[b].T for b in range(B)]).astype(np.float32)
```

```python
import numpy as np
import ml_dtypes
import concourse.bass as bass
import concourse.mybir as mybir
from concourse.bass_utils import run_bass_kernel_spmd

F32 = mybir.dt.float32
BF16 = mybir.dt.bfloat16
AF = mybir.ActivationFunctionType
ALU = mybir.AluOpType

FUSED = True

DEPTH = 4
T = 2048
D = 1024
TT = 512
NTT = 4
EPS = 1e-6
NSLOT = 6
SLOT_N = 1024

P_LN1G, P_LN2G, P_BG, P_CVB, P_CVLG, P_CVLB, P_CVW, P_QG, P_KG, P_FW, P_FB, NP = (
    0, 8, 16, 40, 44, 48, 52, 176, 177, 178, 310, 354)
C_ID, C_U, C_M2, C_M1, C_ONE, C_BD, NC_ = 0, 128, 256, 384, 512, 640, 768

FFN_TILES = [(0, 510), (510, 510), (1020, 510), (1530, 510), (2040, 8)]


class Unit:
    __slots__ = ("w", "r")

    def __init__(self):
        self.w = None
        self.r = {}


class KB:
    def __init__(self, nc):
        self.nc = nc
        self.eng = {"pe": nc.tensor, "act": nc.scalar, "dve": nc.vector, "pool": nc.gpsimd, "sp": nc.sync}
        self.semh = {}
        self.cnt = {}
        self.seen = {e: {} for e in self.eng}
        for e in self.eng:
            self.semh[e] = nc.alloc_semaphore(name="s_" + e)
            self.cnt[e] = 0
        self.pool_pending = {}

    def new_sem(self, name):
        self.semh[name] = self.nc.alloc_semaphore(name="s_" + name)
        self.cnt[name] = 0

    def _deps(self, reads, writes):
        need = {}
        for u in reads:
            if u.w is not None:
                s, v = u.w
                if v > need.get(s, 0):
                    need[s] = v
        for u in writes:
            if u.w is not None:
                s, v = u.w
                if v > need.get(s, 0):
                    need[s] = v
            for s, v in u.r.items():
                if v > need.get(s, 0):
                    need[s] = v
        return need

    def _wait(self, e, need):
        seen = self.seen[e]
        for s, v in need.items():
            if e == "pe" and s == "pe":
                continue
            if seen.get(s, 0) >= v:
                continue
            self.eng[e].wait_ge(self.semh[s], v)
            seen[s] = v

    def op(self, e, fn, reads=(), writes=()):
        need = self._deps(reads, writes)
        if e == "pool" and self.pool_pending:
            for s, v in self.pool_pending.items():
                if v > need.get(s, 0):
                    need[s] = v
            self.pool_pending = {}
        self._wait(e, need)
        ins = fn(self.eng[e])
        self.cnt[e] += 1
        ins.then_inc(self.semh[e], 1)
        c = self.cnt[e]
        for u in reads:
            u.r[e] = c
        for u in writes:
            u.w = (e, c)
            u.r = {}

    def dma(self, q, sem, out, in_, reads=(), writes=()):
        need = self._deps(reads, writes)
        self._wait(q, need)
        ins = self.eng[q].dma_start(out=out, in_=in_)
        self.cnt[sem] += 16
        ins.then_inc(self.semh[sem], 16)
        c = self.cnt[sem]
        for u in reads:
            u.r[sem] = c
        for u in writes:
            u.w = (sem, c)
            u.r = {}

    def barrier(self):
        comp = ("pe", "act", "dve")
        snap = {e: self.cnt[e] for e in comp + ("pool",)}
        for e in comp:
            need = {s: v for s, v in snap.items() if s != e and v > 0}
            self._wait(e, need)
        self.pool_pending = {e: snap[e] for e in comp if snap[e] > 0}

    def wait_all(self, e, sems):
        self._wait(e, {s: self.cnt[s] for s in sems if self.cnt[s] > 0})


def build_program(NL):
    nc = bass.Bass("TRN2", target_bir_lowering=False)
    dt = nc.dram_tensor
    xT_d = dt("xT", [D, T], F32, kind="ExternalInput").ap()
    cst_d = dt("consts", [128, NC_], F32, kind="ExternalInput").ap()
    win_d = dt("w_in_c", [NL, 52, 128, 1024], F32, kind="ExternalInput").ap()
    wv_d = dt("w_v", [NL, 4, 128, 1024], F32, kind="ExternalInput").ap()
    wabc_d = dt("w_abc", [NL, 3, 8, 128, 512], F32, kind="ExternalInput").ap()
    wout_d = dt("w_out_c", [NL, 8, 128, 1024], F32, kind="ExternalInput").ap()
    wup_d = dt("w_up_c", [NL, 44, 128, 1024], F32, kind="ExternalInput").ap()
    wdn_d = dt("w_dn", [NL, 2, 8, 128, 1408], F32, kind="ExternalInput").ap()
    pvec_d = dt("pvec", [NL, 128, NP], F32, kind="ExternalInput").ap()
    pbc_d = dt("pbc", [NL, 128, 1536], F32, kind="ExternalInput").ap()
    sgw_d = dt("sgw", [NL, 128, 1024], F32, kind="ExternalInput").ap()
    yT_d = dt("yT", [D, T], F32, kind="ExternalOutput").ap()

    kb = KB(nc)
    al = nc.alloc_sbuf_tensor

    xT = al("xT_sb", [128, 8, T], F32)
    hT = al("hT_sb", [128, 8, T + 2], BF16)
    scr = al("scr", [128, 16448], F32)
    ring = al("ring", [128, NSLOT, SLOT_N], BF16)
    tmp2 = al("tmp2", [128, 6656], F32)
    cstb = al("cstb", [128, NC_], BF16)
    onesf = al("onesf", [128, 128], F32)
    pvec = al("pvec_sb", [128, NP], F32)
    qgs = al("qgs", [128, 1], F32)
    negh = al("negh", [128, 1], F32)
    ps = nc.alloc_psum_tensor("ps", [128, 8, 512], F32)

    merged = scr[:, 0:8192].bitcast(BF16).rearrange("p (c t) -> p c t", c=8)
    S1 = scr[:, 8192:8192 + 4160].bitcast(BF16).rearrange("p (c t) -> p c t", c=4)
    S2 = scr[:, 12352:12352 + 4096].bitcast(BF16).rearrange("p (c t) -> p c t", c=4)
    act_ffn = scr[:, 0:11264].bitcast(BF16).rearrange("p (c t) -> p c t", c=11)

    def t2f(off, n):
        return tmp2[:, off:off + n]

    def t2b(off, n_words):
        return tmp2[:, off:off + n_words].bitcast(BF16)

    ident = cstb[:, C_ID:C_ID + 128]
    Umat = cstb[:, C_U:C_U + 128]
    M2b = cstb[:, C_M2:C_M2 + 128]
    M1b = cstb[:, C_M1:C_M1 + 128]
    onesb = cstb[:, C_ONE:C_ONE + 128]
    bd2 = cstb[:, C_BD:C_BD + 128]

    x_u = [[Unit() for _ in range(NTT)] for _ in range(8)]
    h_u = [Unit() for _ in range(NTT)]
    m_u = [Unit() for _ in range(NTT)]
    bank_u = [Unit() for _ in range(8)]
    slot_u = [Unit() for _ in range(NSLOT)]
    cst_u = Unit()
    onesf_u = Unit()
    pv_u = Unit()
    for s in range(NSLOT):
        kb.new_sem("w%d" % s)
    for nm in ("ld", "lc", "st", "pv", "pb", "sw"):
        kb.new_sem(nm)

    def bank(i):
        return ps[:, i, :]

    def h_units(lo, hi):
        lo = max(lo, 0)
        return [h_u[i] for i in range(lo // TT, min((hi - 1) // TT, NTT - 1) + 1)]

    tasks = []

    def task(entries, fn):
        tasks.append((entries, fn))

    def mm_group(out_ap, pairs, reads, writes):
        def fn(pe):
            n = len(pairs)
            ins = None
            for i, (l_, r_) in enumerate(pairs):
                ins = pe.matmul(out_ap, lhsT=l_, rhs=r_, start=(i == 0), stop=(i == n - 1))
            return ins
        kb.op("pe", fn, reads, writes)

    def A(e, fn, reads=(), writes=()):
        kb.op(e, fn, reads, writes)

    bank_rr = [0]

    def next_bank(lo=0, hi=8):
        b = lo + bank_rr[0] % (hi - lo)
        bank_rr[0] += 1
        return b

    def setup(slots, sus):
        kb.dma("pool", "lc", cstb[:, :], cst_d[:, :], writes=[cst_u])
        kb.dma("sp", "ld", onesf[:, :], cst_d[:, C_ONE:C_ONE + 128], writes=[onesf_u])
        for c in range(8):
            kb.dma("sp", "ld", xT[:, c, :], xT_d[c * 128:(c + 1) * 128, :], writes=x_u[c])
        fin = ("ld", kb.cnt["ld"])
        cst_u.w = ("lc", kb.cnt["lc"])
        onesf_u.w = fin
        for c in range(8):
            for u in x_u[c]:
                u.w = fin
        A("dve", lambda e: e.memset(hT[:, :, 0:2], 0.0), writes=[h_u[0]])
        A("dve", lambda e: e.memset(negh[:, :], -0.5), writes=[cst_u])

    task([], setup)

    def emit_rmsnorm(l, gcol):
        def fn_tt(tt):
            def fn(slots, sus):
                sqb = t2b((tt % 2) * 2048, 2048).rearrange("p (c t) -> p c t", c=8)
                lnt = t2f(4096 + (tt % 2) * 1024, 512)
                rstd = t2f(4608 + (tt % 2) * 1024, 512)
                u_sq, u_ln, u_rs = usq[tt % 2], uu[tt % 2][0], uu[tt % 2][1]
                cs = slice(tt * TT, (tt + 1) * TT)
                xu = [x_u[c][tt] for c in range(8)]
                A("act", lambda e: e.activation(out=sqb, in_=xT[:, :, cs], func=AF.Square),
                  reads=xu, writes=[u_sq])
                b = next_bank()
                mm_group(bank(b), [(onesb, sqb[:, c, :]) for c in range(8)],
                         reads=[u_sq, cst_u], writes=[bank_u[b]])
                A("act", lambda e: e.activation(out=lnt, in_=bank(b), func=AF.Ln, scale=1.0 / D, bias=EPS),
                  reads=[bank_u[b]], writes=[u_ln])
                A("act", lambda e: e.activation(out=rstd, in_=lnt, func=AF.Exp, scale=-0.5),
                  reads=[u_ln], writes=[u_rs])
                for c in range(8):
                    A("dve", lambda e, c=c: e.scalar_tensor_tensor(
                        out=hT[:, c, 2 + tt * TT:2 + (tt + 1) * TT], in0=xT[:, c, cs],
                        scalar=pvec[:, gcol + c:gcol + c + 1], in1=rstd, op0=ALU.mult, op1=ALU.mult),
                      reads=[x_u[c][tt], u_rs, pv_u], writes=[h_u[tt]])
            return fn
        uu = [(Unit(), Unit()) for _ in range(2)]
        usq = [Unit(), Unit()]
        for tt in range(NTT):
            task([], fn_tt(tt))

    def emit_branch_out(l, br, src, first, sig_off, alias=()):
        src_u = emit_branch_out.src_units
        sig_units = [Unit(), Unit()]
        tmp_units = [Unit(), Unit()]
        for j in range(8):
            def fn(slots, sus, j=j):
                wb, wg = slots
                wb = wb[:, 0:512].rearrange("p (k c) -> p k c", k=4)
                wg = wg[:, 0:1024].rearrange("p (k c) -> p k c", k=8)
                su_b, su_g = sus
                for tt in range(NTT):
                    cs = slice(tt * TT, (tt + 1) * TT)
                    by = next_bank()
                    mm_group(bank(by), [(wb[:, k, :], src[:, k, cs]) for k in range(4)],
                             reads=[su_b] + [src_u[k][tt] for k in range(4)], writes=[bank_u[by]])
                    bg = next_bank()
                    mm_group(bank(bg), [(wg[:, k, :], hT[:, k, 2 + tt * TT:2 + (tt + 1) * TT]) for k in range(8)],
                             reads=[su_g, h_u[tt]], writes=[bank_u[bg]])
                    k2 = (j * NTT + tt) % 2
                    sig = t2f(sig_off + k2 * 512, 512)
                    A("act", lambda e: e.activation(out=sig, in_=bank(bg), func=AF.Sigmoid,
                                                    bias=pvec[:, P_BG + br * 8 + j:P_BG + br * 8 + j + 1]),
                      reads=[bank_u[bg], pv_u], writes=[sig_units[k2]] + list(alias))
                    if first:
                        A("dve", lambda e: e.tensor_tensor(out=merged[:, j, cs], in0=sig, in1=bank(by), op=ALU.mult),
                          reads=[sig_units[k2], bank_u[by]], writes=[m_u[tt]])
                    else:
                        A("dve", lambda e: e.tensor_tensor(out=sig, in0=sig, in1=bank(by), op=ALU.mult),
                          reads=[sig_units[k2], bank_u[by]], writes=[sig_units[k2]])
                        A("dve", lambda e: e.tensor_tensor(out=merged[:, j, cs], in0=merged[:, j, cs], in1=sig,
                                                           op=ALU.add),
                          reads=[sig_units[k2], m_u[tt]], writes=[m_u[tt]])
            task([(wabc_d[l, br, j], 512), (win_d[l, 28 + br * 8 + j], 1024)], fn)

    def emit_attention(l):
        oT = S2
        q_u = [[Unit() for _ in range(NTT)] for _ in range(4)]
        k_u = [[Unit() for _ in range(NTT)] for _ in range(4)]
        v_u = [[Unit() for _ in range(NTT)] for _ in range(4)]
        o_u = [[Unit() for _ in range(NTT)] for _ in range(4)]
        emit_branch_out.src_units = o_u
        sq_u = [Unit() for _ in range(3)]
        ln_u = [Unit() for _ in range(3)]
        zz_u = [Unit(), Unit()]
        rr_u = [Unit(), Unit()]
        bo_u = Unit()
        bo_units = [bo_u, bank_u[7]]
        e_u = [Unit(), Unit()]
        sp_u = [Unit(), Unit()]
        nl_u = [Unit(), Unit()]
        a_u = [Unit(), Unit()]
        s_u = [Unit(), Unit()]
        pc = [0]

        def zz(i):
            return ps[:, 2 * i:2 * i + 2, :]
        rr = ps[:, 4:6, :]

        def qscale():
            A("dve", lambda e: e.tensor_scalar(out=qgs[:, :], in0=pvec[:, P_QG:P_QG + 1], scalar1=0.125, scalar2=None,
                                               op0=ALU.mult),
              reads=[pv_u], writes=[cst_u])
        task([], lambda slots, sus: qscale())

        for hp in range(4):
            for which, chunk0 in (("q", 16), ("k", 20)):
                def fn(slots, sus, which=which, hp=hp):
                    w = slots[0][:, 0:1024].rearrange("p (k c) -> p k c", k=8)
                    su = sus[0]
                    gap = qgs[:, :] if which == "q" else pvec[:, P_KG:P_KG + 1]
                    dst = merged[:, hp, :] if which == "q" else merged[:, 4 + hp, :]
                    du = q_u[hp] if which == "q" else k_u[hp]
                    for tt in range(NTT):
                        cs = slice(tt * TT, (tt + 1) * TT)
                        r3 = pc[0] % 3
                        pc[0] += 1
                        sq_t = t2b(r3 * 256, 256)
                        ln_t = t2f(768 + r3 * 512, 512)
                        b = next_bank()
                        mm_group(bank(b), [(w[:, k, :], hT[:, k, 2 + tt * TT:2 + (tt + 1) * TT]) for k in range(8)],
                                 reads=[su, h_u[tt]], writes=[bank_u[b]])
                        A("act", lambda e: e.activation(out=sq_t, in_=bank(b), func=AF.Square),
                          reads=[bank_u[b]], writes=[sq_u[r3]])
                        b2 = next_bank()
                        mm_group(bank(b2), [(bd2, sq_t)], reads=[sq_u[r3], cst_u], writes=[bank_u[b2]])
                        A("act", lambda e: e.activation(out=ln_t, in_=bank(b2), func=AF.Ln, scale=1.0 / 64, bias=EPS),
                          reads=[bank_u[b2]], writes=[ln_u[r3]])
                        A("act", lambda e: e.activation(out=ln_t, in_=ln_t, func=AF.Exp, scale=-0.5),
                          reads=[ln_u[r3]], writes=[ln_u[r3]])
                        A("dve", lambda e: e.scalar_tensor_tensor(out=dst[:, cs], in0=bank(b), scalar=gap, in1=ln_t,
                                                                  op0=ALU.mult, op1=ALU.mult),
                          reads=[bank_u[b], ln_u[r3], cst_u, pv_u], writes=[du[tt]])
                task([(win_d[l, chunk0 + hp], 1024)], fn)

            def fnv(slots, sus, hp=hp):
                w = slots[0][:, 0:1024].rearrange("p (k c) -> p k c", k=8)
                su = sus[0]
                vvh = S1[:, hp, 0:T].rearrange("p (c d) -> p c d", c=16)
                for tt in range(NTT):
                    b = next_bank()
                    for ci in range(4):
                        c = tt * 4 + ci
                        mm_group(ps[:, b, ci * 128:(ci + 1) * 128],
                                 [(hT[:, k, 2 + c * 128:2 + (c + 1) * 128], w[:, k, :]) for k in range(8)],
                                 reads=[su, h_u[tt]], writes=[bank_u[b]])
                    A("act", lambda e: e.activation(out=vvh[:, tt * 4:(tt + 1) * 4, :],
                                                    in_=bank(b).rearrange("p (c d) -> p c d", c=4), func=AF.Copy),
                      reads=[bank_u[b]], writes=[v_u[hp][tt]])
            task([(win_d[l, 24 + hp], 1024)], fnv)
        task([], lambda slots, sus: kb.barrier())

        if True:
            def fna(slots, sus):
                steps = []
                for hp_ in range(4):
                    for g in range(NTT):
                        top = 4 * g + 3
                        for kb_ in range(top, -1, -1):
                            steps.append((hp_, g, kb_))
                n = len(steps)

                def geom(i):
                    hp, g, kb_ = steps[i]
                    j = kb_ - 4 * g
                    c0 = 128 * j if j > 0 else 0
                    si = i % 2
                    return dict(
                        hp=hp, grp=hp * NTT + g,
                        qT=merged[:, hp, :], kT=merged[:, 4 + hp, :],
                        vv=S1[:, hp, 0:T].rearrange("p (c d) -> p c d", c=16),
                        g=g, kb=kb_, c0=c0, diag=(j >= 0), first=(kb_ == 4 * g + 3), si=si, z=zz(si),
                        eb=t2f(si * 2560, 1024).rearrange("p (h t) -> p h t", h=2),
                        spb=t2b(si * 2560 + 1024, 512).rearrange("p (h t) -> p h t", h=2),
                        nlb=t2b(si * 2560 + 1536, 512).rearrange("p (h t) -> p h t", h=2),
                        ab=t2b(si * 2560 + 2048, 512).rearrange("p (h t) -> p h t", h=2),
                        qcols=slice(g * TT + c0, (g + 1) * TT), kcols=slice(kb_ * 128, (kb_ + 1) * 128),
                        ktt=kb_ // 4)
                Sb = t2b(5120, 512).rearrange("p (h t) -> p h t", h=2)
                m1 = M1b.unsqueeze(1).broadcast_to([128, 2, 128])

                def stage_a(i):
                    G = geom(i)
                    c0, si, z, eb, spb, nlb = G["c0"], G["si"], G["z"], G["eb"], G["spb"], G["nlb"]

                    def qk(pe):
                        ins = None
                        for h in range(2):
                            hs = slice(h * 64, (h + 1) * 64)
                            ins = pe.matmul(z[:, h, c0:512], lhsT=G["kT"][hs, G["kcols"]], rhs=G["qT"][hs, G["qcols"]],
                                            start=True, stop=True)
                        return ins
                    A("pe", qk, reads=[k_u[G["hp"]][G["ktt"]], q_u[G["hp"]][G["g"]]], writes=[zz_u[si]])
                    A("act", lambda e: e.activation(out=eb[:, :, c0:512], in_=z[:, :, c0:512], func=AF.Exp, scale=-1.0),
                      reads=[zz_u[si]], writes=[e_u[si]])
                    A("act", lambda e: e.activation(out=spb[:, :, c0:512], in_=eb[:, :, c0:512], func=AF.Ln,
                                                    scale=1.0, bias=1.0),
                      reads=[e_u[si]], writes=[sp_u[si]])

                def stage_a2(i):
                    G = geom(i)
                    c0, si, z, eb, spb, nlb = G["c0"], G["si"], G["z"], G["eb"], G["spb"], G["nlb"]
                    A("dve", lambda e: e.tensor_tensor(out=nlb[:, :, c0:512], in0=spb[:, :, c0:512],
                                                       in1=z[:, :, c0:512], op=ALU.add),
                      reads=[sp_u[si], zz_u[si]], writes=[nl_u[si]])
                    if G["diag"]:
                        A("dve", lambda e: e.tensor_tensor(out=nlb[:, :, c0:c0 + 128], in0=nlb[:, :, c0:c0 + 128],
                                                           in1=m1, op=ALU.mult),
                          reads=[nl_u[si], cst_u], writes=[nl_u[si]])

                def stage_b(i):
                    G = geom(i)
                    c0, si, eb, spb, nlb, ab = G["c0"], G["si"], G["eb"], G["spb"], G["nlb"], G["ab"]
                    first = G["first"]
                    for h in range(2):
                        def rmm(pe, h=h):
                            pe.matmul(rr[:, h, c0:512], lhsT=Umat, rhs=nlb[:, h, c0:512], start=True, stop=False)
                            if not first:
                                pe.matmul(rr[:, h, c0:512], lhsT=onesb, rhs=Sb[:, h, c0:512], start=False, stop=False)
                            return pe.matmul(rr[:, h, c0:512], lhsT=ident, rhs=spb[:, h, c0:512], start=False, stop=True)
                        A("pe", rmm, reads=[nl_u[si], sp_u[si], s_u[h], cst_u], writes=[rr_u[h]])
                    for h in range(2):
                        if G["kb"] > 0:
                            if first:
                                A("dve", lambda e, h=h: e.memset(Sb[:, h, :], 0.0), writes=[s_u[h]])
                            A("dve", lambda e, h=h: e.tensor_tensor(out=Sb[:, h, c0:512], in0=Sb[:, h, c0:512],
                                                                    in1=nlb[:, h, c0:512], op=ALU.add),
                              reads=[nl_u[si], s_u[h]], writes=[s_u[h]])

                def stage_b2(i):
                    G = geom(i)
                    c0, si, eb, spb, nlb, ab = G["c0"], G["si"], G["eb"], G["spb"], G["nlb"], G["ab"]
                    A("act", lambda e: e.activation(out=ab[:, :, c0:512], in_=rr[:, :, c0:512], func=AF.Exp, scale=-1.0),
                      reads=rr_u, writes=[a_u[si]])
                    if G["diag"]:
                        A("dve", lambda e: e.tensor_tensor(out=ab[:, :, c0:c0 + 128], in0=ab[:, :, c0:c0 + 128],
                                                           in1=m1, op=ALU.mult),
                          reads=[a_u[si], cst_u], writes=[a_u[si]])

                def stage_c(i):
                    G = geom(i)
                    c0, si, ab, g, kb_, hp, vv = G["c0"], G["si"], G["ab"], G["g"], G["kb"], G["hp"], G["vv"]
                    bsel = G["grp"] % 2
                    bi = 6 + bsel

                    def av(pe):
                        ins = None
                        for h in range(2):
                            hs = slice(h * 64, (h + 1) * 64)
                            ins = pe.matmul(ps[hs, bi, c0:512], lhsT=vv[:, kb_, hs], rhs=ab[:, h, c0:512],
                                            start=G["first"], stop=(kb_ == 0), skip_group_check=True)
                        return ins
                    A("pe", av, reads=[a_u[si], v_u[hp][G["ktt"]]], writes=[bo_units[bsel]])
                    if kb_ == 0:
                        A("dve", lambda e: e.tensor_copy(out=oT[:, hp, g * TT:(g + 1) * TT], in_=ps[:, bi, :]),
                          reads=[bo_units[bsel]], writes=[o_u[hp][g]])

                for i in range(n + 2):
                    if i < n:
                        stage_a(i)
                    if 0 <= i - 1 < n:
                        stage_b(i - 1)
                    if i < n:
                        stage_a2(i)
                    if 0 <= i - 1 < n:
                        stage_b2(i - 1)
                    if 0 <= i - 2 < n:
                        stage_c(i - 2)
            task([], fna)

    def emit_sgu(l):
        uT = S1
        vtok = S2.rearrange("p c t -> p (c t)").rearrange("p (c f) -> p c f", c=16)
        u_u = [[Unit() for _ in range(NTT)] for _ in range(4)]
        v_u = [Unit() for _ in range(16)]
        emit_branch_out.src_units = u_u
        pbc = t2f(0, 1536)
        sgw_raw = t2f(1536, 1024).rearrange("p (g t) -> p g t", g=8)
        WmT = t2b(2560, 512).rearrange("p (g t) -> p g t", g=8)
        pb_u, wm_u = Unit(), Unit()
        emit_sgu.alias = [wm_u]
        vg_u = [Unit() for _ in range(4)]
        st_u = [Unit() for _ in range(4)]
        tm_u = [Unit(), Unit()]

        def load_params(slots, sus):
            kb.wait_all("sp", ["pe", "act", "dve", "pool"])
            kb.dma("sp", "pb", pbc, pbc_d[l], writes=[pb_u])
            kb.dma("sp", "sw", t2f(1536, 1024), sgw_d[l], writes=[wm_u])
            A("dve", lambda e: e.tensor_tensor(out=WmT, in0=sgw_raw,
                                               in1=M2b.unsqueeze(1).broadcast_to([128, 8, 128]), op=ALU.mult),
              reads=[wm_u, cst_u], writes=[wm_u])
        task([], load_params)

        for j in range(4):
            def fn(slots, sus, j=j):
                w = slots[0][:, 0:1024].rearrange("p (k c) -> p k c", k=8)
                su = sus[0]
                for tt in range(NTT):
                    b = next_bank()
                    mm_group(bank(b), [(w[:, k, :], hT[:, k, 2 + tt * TT:2 + (tt + 1) * TT]) for k in range(8)],
                             reads=[su, h_u[tt]], writes=[bank_u[b]])
                    A("act", lambda e: e.activation(out=uT[:, j, tt * TT:(tt + 1) * TT], in_=bank(b),
                                                    func=AF.Gelu_apprx_tanh),
                      reads=[bank_u[b]], writes=[u_u[j][tt]])
            task([(win_d[l, j], 1024)], fn)

        def fnv(slots, sus):
            ws = [s_[:, 0:1024].rearrange("p (k f) -> p k f", k=2) for s_ in slots]
            for c in range(16):
                k2 = c % 4
                b = next_bank()
                mm_group(bank(b), [(hT[:, k, 2 + c * 128:2 + (c + 1) * 128], ws[k // 2][:, k % 2, :]) for k in range(8)],
                         reads=list(sus) + [h_u[c // 4]], writes=[bank_u[b]])
                vg = t2f((3072 + k2 * 512) if k2 < 2 else (5184 + (k2 - 2) * 512), 512)
                bst = t2f(4096 + k2 * 16, 6)
                mv = t2f(4096 + k2 * 16 + 6, 2)
                rs = t2f(4096 + k2 * 16 + 8, 1)
                A("act", lambda e: e.activation(out=vg, in_=bank(b), func=AF.Gelu_apprx_tanh),
                  reads=[bank_u[b]], writes=[vg_u[k2]])
                A("dve", lambda e: e.bn_stats(out=bst, in_=vg), reads=[vg_u[k2]], writes=[st_u[k2]])
                A("dve", lambda e: e.bn_aggr(out=mv, in_=bst), reads=[st_u[k2]], writes=[st_u[k2]])
                A("dve", lambda e: e.tensor_scalar(out=rs, in0=mv[:, 1:2], scalar1=EPS, scalar2=None, op0=ALU.add),
                  reads=[st_u[k2]], writes=[st_u[k2]])
                A("pool", lambda e: e.tensor_tensor(out=rs, in0=rs, in1=negh[:, :], op=ALU.pow),
                  reads=[st_u[k2], cst_u], writes=[st_u[k2]])
                A("dve", lambda e: e.tensor_scalar(out=vg, in0=vg, scalar1=mv[:, 0:1], scalar2=rs,
                                                   op0=ALU.subtract, op1=ALU.mult),
                  reads=[vg_u[k2], st_u[k2]], writes=[vg_u[k2]])
                A("dve", lambda e: e.tensor_tensor(out=vg, in0=vg, in1=pbc[:, 0:512], op=ALU.mult),
                  reads=[vg_u[k2], pb_u], writes=[vg_u[k2]])
                A("dve", lambda e: e.tensor_tensor(out=vtok[:, c, :], in0=vg, in1=pbc[:, 512:1024], op=ALU.add),
                  reads=[vg_u[k2], pb_u], writes=[v_u[c]])
        task([(wv_d[l, e_], 1024) for e_ in range(4)], fnv)

        def fng(slots, sus):
            sgb = pbc[:, 1024:1536].rearrange("p (g t) -> p g t", g=4)
            for tt in range(NTT):
                for gp in range(4):
                    b = next_bank()

                    def sg(pe):
                        ins = None
                        for ci in range(4):
                            c = tt * 4 + ci
                            for gh in range(2):
                                g = gp * 2 + gh
                                ins = pe.matmul(ps[gh * 64:(gh + 1) * 64, b, ci * 128:(ci + 1) * 128],
                                                lhsT=vtok[:, c, g * 64:(g + 1) * 64], rhs=WmT[:, g, :],
                                                start=True, stop=True)
                        return ins
                    A("pe", sg, reads=[v_u[tt * 4 + ci] for ci in range(4)] + [wm_u], writes=[bank_u[b]])
                    k2 = (tt * 4 + gp) % 2
                    tm = t2f(4160 + k2 * 512, 512)
                    A("dve", lambda e: e.tensor_tensor(
                        out=tm.rearrange("p (c t) -> p c t", c=4), in0=bank(b).rearrange("p (c t) -> p c t", c=4),
                        in1=sgb[:, gp, :].unsqueeze(1).broadcast_to([128, 4, 128]), op=ALU.add),
                      reads=[bank_u[b], pb_u], writes=[tm_u[k2]])
                    cs = slice(tt * TT, (tt + 1) * TT)
                    A("dve", lambda e: e.tensor_tensor(out=uT[:, gp, cs], in0=uT[:, gp, cs], in1=tm, op=ALU.mult),
                      reads=[tm_u[k2], u_u[gp][tt]], writes=[u_u[gp][tt]])
        task([], fng)

    def emit_conv(l):
        c0T = S1
        cT = S2
        c0_u = [[Unit() for _ in range(NTT)] for _ in range(4)]
        c_u = [[Unit() for _ in range(NTT)] for _ in range(4)]
        emit_branch_out.src_units = c_u
        sg_u = [Unit(), Unit()]

        def fpad(slots, sus):
            A("dve", lambda e: e.memset(c0T[:, :, 0:30], 0.0), writes=[c0_u[j][0] for j in range(4)])
        task([], fpad)
        for j in range(4):
            def fn(slots, sus, j=j):
                wp = slots[0][:, 0:1024].rearrange("p (k c) -> p k c", k=8)
                wg = slots[1][:, 0:1024].rearrange("p (k c) -> p k c", k=8)
                sup, sug = sus
                for tt in range(NTT):
                    hcs = slice(2 + tt * TT, 2 + (tt + 1) * TT)
                    bp = next_bank()
                    mm_group(bank(bp), [(wp[:, k, :], hT[:, k, hcs]) for k in range(8)],
                             reads=[sup, h_u[tt]], writes=[bank_u[bp]])
                    bg = next_bank()
                    mm_group(bank(bg), [(wg[:, k, :], hT[:, k, hcs]) for k in range(8)],
                             reads=[sug, h_u[tt]], writes=[bank_u[bg]])
                    k2 = (j * NTT + tt) % 2
                    sg = t2f(5056 + k2 * 512, 512)
                    A("act", lambda e: e.activation(out=sg, in_=bank(bg), func=AF.Sigmoid),
                      reads=[bank_u[bg]], writes=[sg_u[k2]])
                    A("dve", lambda e: e.tensor_tensor(out=c0T[:, j, 30 + tt * TT:30 + (tt + 1) * TT], in0=bank(bp),
                                                       in1=sg, op=ALU.mult),
                      reads=[bank_u[bp], sg_u[k2]], writes=[c0_u[j][tt]])
            task([(win_d[l, 8 + j], 1024), (win_d[l, 12 + j], 1024)], fn)
        task([], lambda slots, sus: kb.barrier())

        diag = t2b(0, 1984).rearrange("p (k c) -> p k c", k=31)
        c1 = t2f(1984, 2048).rearrange("p (j t) -> p j t", j=4)
        sqc = t2b(4032, 1024).rearrange("p (j t) -> p j t", j=4)
        mu = t2f(5056, 512)
        var = t2f(5568, 512)
        msq = t2f(6080, 512)
        dg_u, c1_u, sqc_u, mu_u, var_u, msq_u = [Unit(), Unit()], [Unit() for _ in range(4)], Unit(), Unit(), Unit(), Unit()
        emit_conv.alias = [sqc_u]

        def build_diag(j, hf, k0, k1):
            A("dve", lambda e: e.tensor_tensor(
                out=diag[:, k0:k1, :], in0=ident.unsqueeze(1).broadcast_to([128, k1 - k0, 128]),
                in1=pvec[:, P_CVW + j * 31 + k0:P_CVW + j * 31 + k1].unsqueeze(2).broadcast_to([128, k1 - k0, 128]),
                op=ALU.mult),
              reads=[cst_u, pv_u], writes=[dg_u[hf]])

        def fnc(slots, sus):
            for tt in range(NTT):
                for j in range(4):
                    b = next_bank(0, 4)
                    rd = [c0_u[j][tt]] + ([c0_u[j][tt - 1]] if tt > 0 else [])
                    for hf, (k0, k1) in enumerate(((0, 16), (16, 31))):
                        if not (tt > 0 and j == 0):
                            build_diag(j, hf, k0, k1)

                        def cmm(pe, k0=k0, k1=k1):
                            ins = None
                            for k in range(k0, k1):
                                ins = pe.matmul(bank(b), lhsT=diag[:, k, :], rhs=c0T[:, j, tt * TT + k:tt * TT + k + TT],
                                                start=(k == 0), stop=(k == 30))
                            return ins
                        A("pe", cmm, reads=rd + [dg_u[hf]], writes=[bank_u[b]])
                    cvb = pvec[:, P_CVB + j:P_CVB + j + 1]
                    A("act", lambda e: e.activation(out=c1[:, j, :], in_=bank(b), func=AF.Identity, bias=cvb),
                      reads=[bank_u[b], pv_u], writes=[c1_u[j]])
                    A("act", lambda e: e.activation(out=sqc[:, j, :], in_=bank(b), func=AF.Square, bias=cvb),
                      reads=[bank_u[b], pv_u], writes=[sqc_u])
                bm, bv = 4, 5
                mm_group(bank(bm), [(onesf[:, :], c1[:, j, :]) for j in range(4)],
                         reads=c1_u + [onesf_u], writes=[bank_u[bm]])
                mm_group(bank(bv), [(onesb, sqc[:, j, :]) for j in range(4)],
                         reads=[sqc_u, cst_u], writes=[bank_u[bv]])
                if tt + 1 < NTT:
                    build_diag(0, 0, 0, 16)
                    build_diag(0, 1, 16, 31)
                A("dve", lambda e: e.tensor_scalar(out=mu, in0=bank(bm), scalar1=1.0 / 512, scalar2=None, op0=ALU.mult),
                  reads=[bank_u[bm]], writes=[mu_u])
                A("dve", lambda e: e.tensor_tensor(out=msq, in0=mu, in1=mu, op=ALU.mult),
                  reads=[mu_u], writes=[msq_u])
                A("dve", lambda e: e.scalar_tensor_tensor(out=var, in0=bank(bv), scalar=1.0 / 512, in1=msq,
                                                          op0=ALU.mult, op1=ALU.subtract),
                  reads=[bank_u[bv], msq_u], writes=[var_u])
                A("act", lambda e: e.activation(out=var, in_=var, func=AF.Ln, scale=1.0, bias=EPS),
                  reads=[var_u], writes=[var_u])
                A("act", lambda e: e.activation(out=var, in_=var, func=AF.Exp, scale=-0.5),
                  reads=[var_u], writes=[var_u])
                for j in range(4):
                    A("dve", lambda e, j=j: e.tensor_tensor(out=c1[:, j, :], in0=c1[:, j, :], in1=mu, op=ALU.subtract),
                      reads=[c1_u[j], mu_u], writes=[c1_u[j]])
                    A("dve", lambda e, j=j: e.tensor_tensor(out=c1[:, j, :], in0=c1[:, j, :], in1=var, op=ALU.mult),
                      reads=[c1_u[j], var_u], writes=[c1_u[j]])
                    A("act", lambda e, j=j: e.activation(
                        out=cT[:, j, tt * TT:(tt + 1) * TT], in_=c1[:, j, :], func=AF.Silu,
                        scale=pvec[:, P_CVLG + j:P_CVLG + j + 1], bias=pvec[:, P_CVLB + j:P_CVLB + j + 1]),
                      reads=[c1_u[j], pv_u], writes=[c_u[j][tt]])
        task([], fnc)

    def emit_outproj(l):
        for j in range(8):
            def fn(slots, sus, j=j):
                w = slots[0][:, 0:1024].rearrange("p (k c) -> p k c", k=8)
                su = sus[0]
                for tt in range(NTT):
                    cs = slice(tt * TT, (tt + 1) * TT)
                    b = next_bank()
                    mm_group(bank(b), [(w[:, k, :], merged[:, k, cs]) for k in range(8)],
                             reads=[su, m_u[tt]], writes=[bank_u[b]])
                    A("dve", lambda e: e.tensor_tensor(out=xT[:, j, cs], in0=xT[:, j, cs], in1=bank(b), op=ALU.add),
                      reads=[bank_u[b], x_u[j][tt]], writes=[x_u[j][tt]])
            task([(wout_d[l, j], 1024)], fn)

    def emit_ffn(l):
        a_u = [[Unit() for _ in range(NTT)] for _ in range(11)]
        tg_u = [Unit(), Unit()]
        tv_u = [Unit(), Unit()]

        def a_units(jj, lo, hi):
            return [a_u[jj][i] for i in range(lo // TT, (hi - 1) // TT + 1)]
        for grp in range(2):
            for jj in range(11):
                j = grp * 11 + jj

                def fn(slots, sus, j=j, jj=jj):
                    wg = slots[0][:, 0:1024].rearrange("p (k c) -> p k c", k=8)
                    wv = slots[1][:, 0:1024].rearrange("p (k c) -> p k c", k=8)
                    sug, suv = sus
                    for ti, (st, ln) in enumerate(FFN_TILES):
                        nn = ln + 2
                        k2 = ti % 2
                        hu = h_units(st - 2, st + ln)
                        bg = next_bank()
                        mm_group(ps[:, bg, 0:nn], [(wg[:, k, :], hT[:, k, st:st + nn]) for k in range(8)],
                                 reads=[sug] + hu, writes=[bank_u[bg]])
                        bv = next_bank()
                        mm_group(ps[:, bv, 0:nn], [(wv[:, k, :], hT[:, k, st:st + nn]) for k in range(8)],
                                 reads=[suv] + hu, writes=[bank_u[bv]])
                        tg = t2f(k2 * 1024, 512)[:, 0:ln]
                        tv = t2f(k2 * 1024 + 512, 512)[:, 0:ln]
                        for (bk, tbuf, tu, ch) in ((bg, tg, tg_u[k2], j), (bv, tv, tv_u[k2], 22 + j)):
                            def wcol(k, ch=ch):
                                return pvec[:, P_FW + k * 44 + ch:P_FW + k * 44 + ch + 1]
                            fb = pvec[:, P_FB + ch:P_FB + ch + 1]
                            A("act", lambda e, bk=bk, tbuf=tbuf, wcol=wcol, fb=fb: e.activation(
                                out=tbuf, in_=ps[:, bk, 2:nn], func=AF.Identity, scale=wcol(2), bias=fb),
                              reads=[bank_u[bk], pv_u], writes=[tu])
                            A("dve", lambda e, bk=bk, tbuf=tbuf, wcol=wcol: e.scalar_tensor_tensor(
                                out=tbuf, in0=ps[:, bk, 1:nn - 1], scalar=wcol(1), in1=tbuf, op0=ALU.mult, op1=ALU.add),
                              reads=[bank_u[bk], pv_u, tu], writes=[tu])
                            A("dve", lambda e, bk=bk, tbuf=tbuf, wcol=wcol: e.scalar_tensor_tensor(
                                out=tbuf, in0=ps[:, bk, 0:nn - 2], scalar=wcol(0), in1=tbuf, op0=ALU.mult, op1=ALU.add),
                              reads=[bank_u[bk], pv_u, tu], writes=[tu])
                        A("act", lambda e: e.activation(out=tg, in_=tg, func=AF.Silu),
                          reads=[tg_u[k2]], writes=[tg_u[k2]])
                        A("dve", lambda e: e.tensor_tensor(out=act_ffn[:, jj, st:st + ln], in0=tg, in1=tv, op=ALU.mult),
                          reads=[tg_u[k2], tv_u[k2]], writes=a_units(jj, st, st + ln))
                task([(wup_d[l, j], 1024), (wup_d[l, 22 + j], 1024)], fn)
            for dj in range(8):
                def fnd(slots, sus, dj=dj):
                    w0 = slots[0][:, 0:768].rearrange("p (k c) -> p k c", k=6)
                    w1 = slots[1][:, 0:640].rearrange("p (k c) -> p k c", k=5)
                    su0, su1 = sus
                    for tt in range(NTT):
                        cs = slice(tt * TT, (tt + 1) * TT)
                        b = next_bank()
                        pairs = [(w0[:, k, :], act_ffn[:, k, cs]) for k in range(6)] + \
                                [(w1[:, k, :], act_ffn[:, 6 + k, cs]) for k in range(5)]
                        mm_group(bank(b), pairs, reads=[su0, su1] + [a_u[k][tt] for k in range(11)],
                                 writes=[bank_u[b]])
                        A("dve", lambda e: e.tensor_tensor(out=xT[:, dj, cs], in0=xT[:, dj, cs], in1=bank(b), op=ALU.add),
                          reads=[bank_u[b], x_u[dj][tt]], writes=[x_u[dj][tt]])
                task([(wdn_d[l, grp, dj][:, 0:768], 768), (wdn_d[l, grp, dj][:, 768:1408], 640)], fnd)

    def emit_barrier():
        task([], lambda slots, sus: kb.barrier())

    for l in range(NL):
        task([], lambda slots, sus, l=l: kb.dma("sp", "pv", pvec[:, :], pvec_d[l], writes=[pv_u]))
        emit_rmsnorm(l, P_LN1G)
        emit_barrier()
        emit_attention(l)
        emit_barrier()
        emit_branch_out(l, 2, S2, True, 0)
        emit_barrier()
        emit_sgu(l)
        emit_branch_out(l, 0, S1[:, :, 0:T], False, 1536, alias=emit_sgu.alias)
        emit_barrier()
        emit_conv(l)
        emit_branch_out(l, 1, S2, False, 4032, alias=emit_conv.alias)
        emit_outproj(l)
        emit_barrier()
        emit_rmsnorm(l, P_LN2G)
        emit_barrier()
        emit_ffn(l)
        emit_barrier()

    def store(slots, sus):
        for c in range(8):
            kb.dma("sp", "st", yT_d[c * 128:(c + 1) * 128, :], xT[:, c, :], reads=x_u[c])
        kb.wait_all("sp", ["st"])
    task([], store)

    ent = []
    first_ent = []
    for (es, fn) in tasks:
        first_ent.append(len(ent))
        ent.extend(es)
    next_dma = [0]

    def issue(m):
        s = m % NSLOT
        ap, n = ent[m]
        kb.dma("pool", "w%d" % s, ring[:, s, 0:n], ap, writes=[slot_u[s]])

    LOOK = NSLOT
    for ti, (es, fn) in enumerate(tasks):
        f0 = first_ent[ti]
        last = f0 + len(es) - 1
        assert len(es) <= NSLOT
        while next_dma[0] < len(ent) and next_dma[0] - NSLOT < f0 and next_dma[0] <= max(last, f0 + LOOK - 1):
            issue(next_dma[0])
            next_dma[0] += 1
        assert next_dma[0] > last
        fn([ring[:, (f0 + i) % NSLOT, :] for i in range(len(es))],
           [slot_u[(f0 + i) % NSLOT] for i in range(len(es))])
    return nc


def _consts():
    c = np.zeros((128, NC_), np.float32)
    i = np.arange(128)
    c[:, C_ID:C_ID + 128] = np.eye(128)
    c[:, C_U:C_U + 128] = (i[:, None] > i[None, :])
    c[:, C_M2:C_M2 + 128] = (i[:, None] <= i[None, :])
    c[:, C_M1:C_M1 + 128] = (i[:, None] < i[None, :])
    c[:, C_ONE:C_ONE + 128] = 1.0
    c[:, C_BD:C_BD + 128] = ((i[:, None] // 64) == (i[None, :] // 64))
    return c


def _chunked(w, kc):
    ncol = w.shape[1]
    return np.ascontiguousarray(
        w.reshape(kc, 128, ncol // 128, 128).transpose(2, 1, 0, 3)).reshape(ncol // 128, 128, kc * 128)


def _prep_layer(inp, l):
    f = lambda a: np.asarray(a, dtype=np.float32)
    w_in = f(inp["w_in"][l])
    out = {}
    out["w_in_c"] = _chunked(w_in, 8)
    wv = w_in[:, 512:1024]
    out["w_v"] = np.ascontiguousarray(wv.reshape(4, 2, 128, 512).transpose(0, 2, 1, 3)).reshape(4, 128, 1024)
    out["w_abc"] = np.stack([_chunked(f(inp[k][l]), 4) for k in ("w_a_out", "w_b_out", "w_c_out")])
    out["w_out_c"] = _chunked(f(inp["w_out"][l]), 8)
    out["w_up_c"] = _chunked(f(inp["w_up"][l]), 8)
    wd = f(inp["w_down"][l])
    out["w_dn"] = np.ascontiguousarray(wd.reshape(2, 11, 128, 8, 128).transpose(0, 3, 2, 1, 4)).reshape(2, 8, 128, 1408)
    pv = np.zeros((128, NP), np.float32)
    pv[:, P_LN1G:P_LN1G + 8] = f(inp["ln1_g"][l]).reshape(8, 128).T
    pv[:, P_LN2G:P_LN2G + 8] = f(inp["ln2_g"][l]).reshape(8, 128).T
    pv[:, P_BG:P_BG + 24] = f(inp["b_gate"][l]).reshape(3, 8, 128).transpose(2, 0, 1).reshape(128, 24)
    pv[:, P_CVB:P_CVB + 4] = f(inp["cv_b"][l]).reshape(4, 128).T
    pv[:, P_CVLG:P_CVLG + 4] = f(inp["cv_ln_g"][l]).reshape(4, 128).T
    pv[:, P_CVLB:P_CVLB + 4] = f(inp["cv_ln_b"][l]).reshape(4, 128).T
    pv[:, P_CVW:P_CVW + 124] = f(inp["cv_w"][l]).reshape(31, 4, 128).transpose(2, 1, 0).reshape(128, 124)
    pv[:, P_QG] = np.tile(f(inp["q_norm_g"][l]), 2)
    pv[:, P_KG] = np.tile(f(inp["k_norm_g"][l]), 2)
    pv[:, P_FW:P_FW + 132] = f(inp["ffn_conv_w"][l]).reshape(3, 44, 128).transpose(2, 0, 1).reshape(128, 132)
    pv[:, P_FB:P_FB + 44] = f(inp["ffn_conv_b"][l]).reshape(44, 128).T
    out["pvec"] = pv
    pb = np.zeros((128, 1536), np.float32)
    pb[:, 0:512] = f(inp["sg_ln_g"][l])[None, :]
    pb[:, 512:1024] = f(inp["sg_ln_b"][l])[None, :]
    sgb = f(inp["sg_b"][l])
    pb[:, 1024:1536] = np.repeat(sgb.reshape(4, 2, 128), 64, axis=1).transpose(1, 0, 2).reshape(128, 512)
    out["pbc"] = pb
    out["sgw"] = np.ascontiguousarray(f(inp["sg_w"][l]).transpose(2, 0, 1)).reshape(128, 1024)
    return out


_PROG = {}


def _get_prog(nl):
    if nl not in _PROG:
        _PROG[nl] = build_program(nl)
    return _PROG[nl]


def kernel(**inputs):
    x = np.asarray(inputs["x"], dtype=np.float32)
    B = x.shape[0]
    layers = [_prep_layer(inputs, l) for l in range(DEPTH)]
    cst = _consts()
    xT = [np.ascontiguousarray(x[b].T) for b in range(B)]
    groups = [list(range(DEPTH))] if FUSED else [[l] for l in range(DEPTH)]
    for grp in groups:
        nc = _get_prog(len(grp))
        wmaps = {k: np.stack([layers[l][k] for l in grp]) for k in layers[0]}
        in_maps = []
        for b in range(B):
            m = {"xT": xT[b], "consts": cst}
            m.update(wmaps)
            in_maps.append(m)
        res = run_bass_kernel_spmd(nc, in_maps, core_ids=list(range(B)))
        xT = [np.asarray(res.results[b]["yT"], dtype=np.float32) for b in range(B)]
    return np.stack([xT[b].T for b in range(B)]).astype(np.float32)
```

```python
import numpy as np
import ml_dtypes
import concourse.bass as bass
import concourse.mybir as mybir
from concourse.bass_utils import run_bass_kernel_spmd

F32 = mybir.dt.float32
BF16 = mybir.dt.bfloat16
AF = mybir.ActivationFunctionType
ALU = mybir.AluOpType

FUSED = True

DEPTH = 4
T = 2048
D = 1024
TT = 512
NTT = 4
EPS = 1e-6
NSLOT = 6
SLOT_N = 1024

P_LN1G, P_LN2G, P_BG, P_CVB, P_CVLG, P_CVLB, P_CVW, P_QG, P_KG, P_FW, P_FB, NP = (
    0, 8, 16, 40, 44, 48, 52, 176, 177, 178, 310, 354)
C_ID, C_U, C_M2, C_M1, C_ONE, C_BD, NC_ = 0, 128, 256, 384, 512, 640, 768

FFN_TILES = [(0, 510), (510, 510), (1020, 510), (1530, 510), (2040, 8)]


class Unit:
    __slots__ = ("w", "r")

    def __init__(self):
        self.w = None
        self.r = {}


class KB:
    def __init__(self, nc):
        self.nc = nc
        self.eng = {"pe": nc.tensor, "act": nc.scalar, "dve": nc.vector, "pool": nc.gpsimd, "sp": nc.sync}
        self.semh = {}
        self.cnt = {}
        self.seen = {e: {} for e in self.eng}
        for e in self.eng:
            self.semh[e] = nc.alloc_semaphore(name="s_" + e)
            self.cnt[e] = 0
        self.pool_pending = {}

    def new_sem(self, name):
        self.semh[name] = self.nc.alloc_semaphore(name="s_" + name)
        self.cnt[name] = 0

    def _deps(self, reads, writes):
        need = {}
        for u in reads:
            if u.w is not None:
                s, v = u.w
                if v > need.get(s, 0):
                    need[s] = v
        for u in writes:
            if u.w is not None:
                s, v = u.w
                if v > need.get(s, 0):
                    need[s] = v
            for s, v in u.r.items():
                if v > need.get(s, 0):
                    need[s] = v
        return need

    def _wait(self, e, need):
        seen = self.seen[e]
        for s, v in need.items():
            if e == "pe" and s == "pe":
                continue
            if seen.get(s, 0) >= v:
                continue
            self.eng[e].wait_ge(self.semh[s], v)
            seen[s] = v

    def op(self, e, fn, reads=(), writes=()):
        need = self._deps(reads, writes)
        if e == "pool" and self.pool_pending:
            for s, v in self.pool_pending.items():
                if v > need.get(s, 0):
                    need[s] = v
            self.pool_pending = {}
        self._wait(e, need)
        ins = fn(self.eng[e])
        self.cnt[e] += 1
        ins.then_inc(self.semh[e], 1)
        c = self.cnt[e]
        for u in reads:
            u.r[e] = c
        for u in writes:
            u.w = (e, c)
            u.r = {}

    def dma(self, q, sem, out, in_, reads=(), writes=()):
        need = self._deps(reads, writes)
        self._wait(q, need)
        ins = self.eng[q].dma_start(out=out, in_=in_)
        self.cnt[sem] += 16
        ins.then_inc(self.semh[sem], 16)
        c = self.cnt[sem]
        for u in reads:
            u.r[sem] = c
        for u in writes:
            u.w = (sem, c)
            u.r = {}

    def barrier(self):
        comp = ("pe", "act", "dve")
        snap = {e: self.cnt[e] for e in comp + ("pool",)}
        for e in comp:
            need = {s: v for s, v in snap.items() if s != e and v > 0}
            self._wait(e, need)
        self.pool_pending = {e: snap[e] for e in comp if snap[e] > 0}

    def wait_all(self, e, sems):
        self._wait(e, {s: self.cnt[s] for s in sems if self.cnt[s] > 0})


def build_program(NL):
    nc = bass.Bass("TRN2", target_bir_lowering=False)
    dt = nc.dram_tensor
    xT_d = dt("xT", [D, T], F32, kind="ExternalInput").ap()
    cst_d = dt("consts", [128, NC_], F32, kind="ExternalInput").ap()
    win_d = dt("w_in_c", [NL, 52, 128, 1024], F32, kind="ExternalInput").ap()
    wv_d = dt("w_v", [NL, 4, 128, 1024], F32, kind="ExternalInput").ap()
    wabc_d = dt("w_abc", [NL, 3, 8, 128, 512], F32, kind="ExternalInput").ap()
    wout_d = dt("w_out_c", [NL, 8, 128, 1024], F32, kind="ExternalInput").ap()
    wup_d = dt("w_up_c", [NL, 44, 128, 1024], F32, kind="ExternalInput").ap()
    wdn_d = dt("w_dn", [NL, 2, 8, 128, 1408], F32, kind="ExternalInput").ap()
    pvec_d = dt("pvec", [NL, 128, NP], F32, kind="ExternalInput").ap()
    pbc_d = dt("pbc", [NL, 128, 1536], F32, kind="ExternalInput").ap()
    sgw_d = dt("sgw", [NL, 128, 1024], F32, kind="ExternalInput").ap()
    yT_d = dt("yT", [D, T], F32, kind="ExternalOutput").ap()

    kb = KB(nc)
    al = nc.alloc_sbuf_tensor

    xT = al("xT_sb", [128, 8, T], F32)
    hT = al("hT_sb", [128, 8, T + 2], BF16)
    scr = al("scr", [128, 16448], F32)
    ring = al("ring", [128, NSLOT, SLOT_N], BF16)
    tmp2 = al("tmp2", [128, 6656], F32)
    cstb = al("cstb", [128, NC_], BF16)
    onesf = al("onesf", [128, 128], F32)
    pvec = al("pvec_sb", [128, NP], F32)
    qgs = al("qgs", [128, 1], F32)
    negh = al("negh", [128, 1], F32)
    ps = nc.alloc_psum_tensor("ps", [128, 8, 512], F32)

    merged = scr[:, 0:8192].bitcast(BF16).rearrange("p (c t) -> p c t", c=8)
    S1 = scr[:, 8192:8192 + 4160].bitcast(BF16).rearrange("p (c t) -> p c t", c=4)
    S2 = scr[:, 12352:12352 + 4096].bitcast(BF16).rearrange("p (c t) -> p c t", c=4)
    act_ffn = scr[:, 0:11264].bitcast(BF16).rearrange("p (c t) -> p c t", c=11)

    def t2f(off, n):
        return tmp2[:, off:off + n]

    def t2b(off, n_words):
        return tmp2[:, off:off + n_words].bitcast(BF16)

    ident = cstb[:, C_ID:C_ID + 128]
    Umat = cstb[:, C_U:C_U + 128]
    M2b = cstb[:, C_M2:C_M2 + 128]
    M1b = cstb[:, C_M1:C_M1 + 128]
    onesb = cstb[:, C_ONE:C_ONE + 128]
    bd2 = cstb[:, C_BD:C_BD + 128]

    x_u = [[Unit() for _ in range(NTT)] for _ in range(8)]
    h_u = [Unit() for _ in range(NTT)]
    m_u = [Unit() for _ in range(NTT)]
    bank_u = [Unit() for _ in range(8)]
    slot_u = [Unit() for _ in range(NSLOT)]
    cst_u = Unit()
    onesf_u = Unit()
    pv_u = Unit()
    for s in range(NSLOT):
        kb.new_sem("w%d" % s)
    for nm in ("ld", "lc", "st", "pv", "pb", "sw", "x0", "x1", "x2", "x3"):
        kb.new_sem(nm)

    def bank(i):
        return ps[:, i, :]

    def h_units(lo, hi):
        lo = max(lo, 0)
        return [h_u[i] for i in range(lo // TT, min((hi - 1) // TT, NTT - 1) + 1)]

    tasks = []

    def task(entries, fn):
        tasks.append((entries, fn))

    def mm_group(out_ap, pairs, reads, writes):
        def fn(pe):
            n = len(pairs)
            ins = None
            for i, (l_, r_) in enumerate(pairs):
                ins = pe.matmul(out_ap, lhsT=l_, rhs=r_, start=(i == 0), stop=(i == n - 1))
            return ins
        kb.op("pe", fn, reads, writes)

    def A(e, fn, reads=(), writes=()):
        kb.op(e, fn, reads, writes)

    bank_rr = [0]

    def next_bank(lo=0, hi=8):
        b = lo + bank_rr[0] % (hi - lo)
        bank_rr[0] += 1
        return b

    def setup(slots, sus):
        kb.dma("pool", "lc", cstb[:, :], cst_d[:, :], writes=[cst_u])
        kb.dma("sp", "ld", onesf[:, :], cst_d[:, C_ONE:C_ONE + 128], writes=[onesf_u])
        for tt in range(NTT):
            sx = "x%d" % tt
            for c in range(8):
                kb.dma("sp", sx, xT[:, c, tt * TT:(tt + 1) * TT],
                       xT_d[c * 128:(c + 1) * 128, tt * TT:(tt + 1) * TT], writes=[x_u[c][tt]])
            for c in range(8):
                x_u[c][tt].w = (sx, kb.cnt[sx])
        cst_u.w = ("lc", kb.cnt["lc"])
        A("dve", lambda e: e.memset(hT[:, :, 0:2], 0.0), writes=[h_u[0]])
        A("dve", lambda e: e.memset(negh[:, :], -0.5), writes=[cst_u])

    task([], setup)

    def emit_rmsnorm(l, gcol):
        def fn_tt(tt):
            def fn(slots, sus):
                sqb = t2b((tt % 2) * 2048, 2048).rearrange("p (c t) -> p c t", c=8)
                lnt = t2f(4096 + (tt % 2) * 1024, 512)
                rstd = t2f(4608 + (tt % 2) * 1024, 512)
                u_sq, u_ln, u_rs = usq[tt % 2], uu[tt % 2][0], uu[tt % 2][1]
                cs = slice(tt * TT, (tt + 1) * TT)
                xu = [x_u[c][tt] for c in range(8)]
                A("act", lambda e: e.activation(out=sqb, in_=xT[:, :, cs], func=AF.Square),
                  reads=xu, writes=[u_sq])
                b = next_bank()
                mm_group(bank(b), [(onesb, sqb[:, c, :]) for c in range(8)],
                         reads=[u_sq, cst_u], writes=[bank_u[b]])
                A("act", lambda e: e.activation(out=lnt, in_=bank(b), func=AF.Ln, scale=1.0 / D, bias=EPS),
                  reads=[bank_u[b]], writes=[u_ln])
                A("act", lambda e: e.activation(out=rstd, in_=lnt, func=AF.Exp, scale=-0.5),
                  reads=[u_ln], writes=[u_rs])
                for c in range(8):
                    A("dve", lambda e, c=c: e.scalar_tensor_tensor(
                        out=hT[:, c, 2 + tt * TT:2 + (tt + 1) * TT], in0=xT[:, c, cs],
                        scalar=pvec[:, gcol + c:gcol + c + 1], in1=rstd, op0=ALU.mult, op1=ALU.mult),
                      reads=[x_u[c][tt], u_rs, pv_u], writes=[h_u[tt]])
            return fn
        uu = [(Unit(), Unit()) for _ in range(2)]
        usq = [Unit(), Unit()]
        for tt in range(NTT):
            task([], fn_tt(tt))

    def emit_branch_out(l, br, src, first, sig_off, alias=()):
        src_u = emit_branch_out.src_units
        sig_units = [Unit(), Unit()]
        tmp_units = [Unit(), Unit()]
        for j in range(8):
            def fn(slots, sus, j=j):
                wb, wg = slots
                wb = wb[:, 0:512].rearrange("p (k c) -> p k c", k=4)
                wg = wg[:, 0:1024].rearrange("p (k c) -> p k c", k=8)
                su_b, su_g = sus
                for tt in range(NTT):
                    cs = slice(tt * TT, (tt + 1) * TT)
                    by = next_bank()
                    mm_group(bank(by), [(wb[:, k, :], src[:, k, cs]) for k in range(4)],
                             reads=[su_b] + [src_u[k][tt] for k in range(4)], writes=[bank_u[by]])
                    bg = next_bank()
                    mm_group(bank(bg), [(wg[:, k, :], hT[:, k, 2 + tt * TT:2 + (tt + 1) * TT]) for k in range(8)],
                             reads=[su_g, h_u[tt]], writes=[bank_u[bg]])
                    k2 = (j * NTT + tt) % 2
                    sig = t2f(sig_off + k2 * 512, 512)
                    A("act", lambda e: e.activation(out=sig, in_=bank(bg), func=AF.Sigmoid,
                                                    bias=pvec[:, P_BG + br * 8 + j:P_BG + br * 8 + j + 1]),
                      reads=[bank_u[bg], pv_u], writes=[sig_units[k2]] + list(alias))
                    if first:
                        A("dve", lambda e: e.tensor_tensor(out=merged[:, j, cs], in0=sig, in1=bank(by), op=ALU.mult),
                          reads=[sig_units[k2], bank_u[by]], writes=[m_u[tt]])
                    else:
                        A("dve", lambda e: e.tensor_tensor(out=sig, in0=sig, in1=bank(by), op=ALU.mult),
                          reads=[sig_units[k2], bank_u[by]], writes=[sig_units[k2]])
                        A("dve", lambda e: e.tensor_tensor(out=merged[:, j, cs], in0=merged[:, j, cs], in1=sig,
                                                           op=ALU.add),
                          reads=[sig_units[k2], m_u[tt]], writes=[m_u[tt]])
            task([(wabc_d[l, br, j], 512), (win_d[l, 28 + br * 8 + j], 1024)], fn)

    def emit_attention(l):
        oT = S2
        q_u = [[Unit() for _ in range(NTT)] for _ in range(4)]
        k_u = [[Unit() for _ in range(NTT)] for _ in range(4)]
        v_u = [[Unit() for _ in range(NTT)] for _ in range(4)]
        o_u = [[Unit() for _ in range(NTT)] for _ in range(4)]
        emit_branch_out.src_units = o_u
        sq_u = [Unit() for _ in range(3)]
        ln_u = [Unit() for _ in range(3)]
        zz_u = [Unit(), Unit()]
        rr_u = [Unit(), Unit()]
        bo_u = Unit()
        bo_units = [bo_u, bank_u[7]]
        e_u = [Unit(), Unit()]
        sp_u = [Unit(), Unit()]
        nl_u = [Unit(), Unit()]
        a_u = [Unit(), Unit()]
        s_u = [Unit(), Unit()]
        pc = [0]

        def zz(i):
            return ps[:, 2 * i:2 * i + 2, :]
        rr = ps[:, 4:6, :]

        def qscale():
            A("dve", lambda e: e.tensor_scalar(out=qgs[:, :], in0=pvec[:, P_QG:P_QG + 1], scalar1=0.125, scalar2=None,
                                               op0=ALU.mult),
              reads=[pv_u], writes=[cst_u])
        task([], lambda slots, sus: qscale())

        for hp in range(4):
            for which, chunk0 in (("q", 16), ("k", 20)):
                def fn(slots, sus, which=which, hp=hp):
                    w = slots[0][:, 0:1024].rearrange("p (k c) -> p k c", k=8)
                    su = sus[0]
                    gap = qgs[:, :] if which == "q" else pvec[:, P_KG:P_KG + 1]
                    dst = merged[:, hp, :] if which == "q" else merged[:, 4 + hp, :]
                    du = q_u[hp] if which == "q" else k_u[hp]
                    for tt in range(NTT):
                        cs = slice(tt * TT, (tt + 1) * TT)
                        r3 = pc[0] % 3
                        pc[0] += 1
                        sq_t = t2b(r3 * 256, 256)
                        ln_t = t2f(768 + r3 * 512, 512)
                        b = next_bank()
                        mm_group(bank(b), [(w[:, k, :], hT[:, k, 2 + tt * TT:2 + (tt + 1) * TT]) for k in range(8)],
                                 reads=[su, h_u[tt]], writes=[bank_u[b]])
                        A("act", lambda e: e.activation(out=sq_t, in_=bank(b), func=AF.Square),
                          reads=[bank_u[b]], writes=[sq_u[r3]])
                        b2 = next_bank()
                        mm_group(bank(b2), [(bd2, sq_t)], reads=[sq_u[r3], cst_u], writes=[bank_u[b2]])
                        A("act", lambda e: e.activation(out=ln_t, in_=bank(b2), func=AF.Ln, scale=1.0 / 64, bias=EPS),
                          reads=[bank_u[b2]], writes=[ln_u[r3]])
                        A("act", lambda e: e.activation(out=ln_t, in_=ln_t, func=AF.Exp, scale=-0.5),
                          reads=[ln_u[r3]], writes=[ln_u[r3]])
                        A("dve", lambda e: e.scalar_tensor_tensor(out=dst[:, cs], in0=bank(b), scalar=gap, in1=ln_t,
                                                                  op0=ALU.mult, op1=ALU.mult),
                          reads=[bank_u[b], ln_u[r3], cst_u, pv_u], writes=[du[tt]])
                task([(win_d[l, chunk0 + hp], 1024)], fn)

            def fnv(slots, sus, hp=hp):
                w = slots[0][:, 0:1024].rearrange("p (k c) -> p k c", k=8)
                su = sus[0]
                vvh = S1[:, hp, 0:T].rearrange("p (c d) -> p c d", c=16)
                for tt in range(NTT):
                    b = next_bank()
                    for ci in range(4):
                        c = tt * 4 + ci
                        mm_group(ps[:, b, ci * 128:(ci + 1) * 128],
                                 [(hT[:, k, 2 + c * 128:2 + (c + 1) * 128], w[:, k, :]) for k in range(8)],
                                 reads=[su, h_u[tt]], writes=[bank_u[b]])
                    A("act", lambda e: e.activation(out=vvh[:, tt * 4:(tt + 1) * 4, :],
                                                    in_=bank(b).rearrange("p (c d) -> p c d", c=4), func=AF.Copy),
                      reads=[bank_u[b]], writes=[v_u[hp][tt]])
            task([(win_d[l, 24 + hp], 1024)], fnv)
        task([], lambda slots, sus: kb.barrier())

        if True:
            def fna(slots, sus):
                steps = []
                for hp_ in range(4):
                    for g in range(NTT):
                        top = 4 * g + 3
                        for kb_ in range(top, -1, -1):
                            steps.append((hp_, g, kb_))
                n = len(steps)

                def geom(i):
                    hp, g, kb_ = steps[i]
                    j = kb_ - 4 * g
                    c0 = 128 * j if j > 0 else 0
                    si = i % 2
                    return dict(
                        hp=hp, grp=hp * NTT + g,
                        qT=merged[:, hp, :], kT=merged[:, 4 + hp, :],
                        vv=S1[:, hp, 0:T].rearrange("p (c d) -> p c d", c=16),
                        g=g, kb=kb_, c0=c0, diag=(j >= 0), first=(kb_ == 4 * g + 3), si=si, z=zz(si),
                        eb=t2f(si * 2560, 1024).rearrange("p (h t) -> p h t", h=2),
                        spb=t2b(si * 2560 + 1024, 512).rearrange("p (h t) -> p h t", h=2),
                        nlb=t2b(si * 2560 + 1536, 512).rearrange("p (h t) -> p h t", h=2),
                        ab=t2b(si * 2560 + 2048, 512).rearrange("p (h t) -> p h t", h=2),
                        qcols=slice(g * TT + c0, (g + 1) * TT), kcols=slice(kb_ * 128, (kb_ + 1) * 128),
                        ktt=kb_ // 4)
                Sb = t2b(5120, 512).rearrange("p (h t) -> p h t", h=2)
                m1 = M1b.unsqueeze(1).broadcast_to([128, 2, 128])

                def stage_a(i):
                    G = geom(i)
                    c0, si, z, eb, spb, nlb = G["c0"], G["si"], G["z"], G["eb"], G["spb"], G["nlb"]

                    def qk(pe):
                        ins = None
                        for h in range(2):
                            hs = slice(h * 64, (h + 1) * 64)
                            ins = pe.matmul(z[:, h, c0:512], lhsT=G["kT"][hs, G["kcols"]], rhs=G["qT"][hs, G["qcols"]],
                                            start=True, stop=True)
                        return ins
                    A("pe", qk, reads=[k_u[G["hp"]][G["ktt"]], q_u[G["hp"]][G["g"]]], writes=[zz_u[si]])
                    A("act", lambda e: e.activation(out=eb[:, :, c0:512], in_=z[:, :, c0:512], func=AF.Exp, scale=-1.0),
                      reads=[zz_u[si]], writes=[e_u[si]])
                    A("act", lambda e: e.activation(out=spb[:, :, c0:512], in_=eb[:, :, c0:512], func=AF.Ln,
                                                    scale=1.0, bias=1.0),
                      reads=[e_u[si]], writes=[sp_u[si]])

                def stage_a2(i):
                    G = geom(i)
                    c0, si, z, eb, spb, nlb = G["c0"], G["si"], G["z"], G["eb"], G["spb"], G["nlb"]
                    A("dve", lambda e: e.tensor_tensor(out=nlb[:, :, c0:512], in0=spb[:, :, c0:512],
                                                       in1=z[:, :, c0:512], op=ALU.add),
                      reads=[sp_u[si], zz_u[si]], writes=[nl_u[si]])
                    if G["diag"]:
                        A("dve", lambda e: e.tensor_tensor(out=nlb[:, :, c0:c0 + 128], in0=nlb[:, :, c0:c0 + 128],
                                                           in1=m1, op=ALU.mult),
                          reads=[nl_u[si], cst_u], writes=[nl_u[si]])

                def stage_b(i):
                    G = geom(i)
                    c0, si, eb, spb, nlb, ab = G["c0"], G["si"], G["eb"], G["spb"], G["nlb"], G["ab"]
                    first = G["first"]
                    for h in range(2):
                        def rmm(pe, h=h):
                            pe.matmul(rr[:, h, c0:512], lhsT=Umat, rhs=nlb[:, h, c0:512], start=True, stop=False)
                            if not first:
                                pe.matmul(rr[:, h, c0:512], lhsT=onesb, rhs=Sb[:, h, c0:512], start=False, stop=False)
                            return pe.matmul(rr[:, h, c0:512], lhsT=ident, rhs=spb[:, h, c0:512], start=False, stop=True)
                        A("pe", rmm, reads=[nl_u[si], sp_u[si], s_u[h], cst_u], writes=[rr_u[h]])
                    for h in range(2):
                        if G["kb"] > 0:
                            if first:
                                A("dve", lambda e, h=h: e.memset(Sb[:, h, :], 0.0), writes=[s_u[h]])
                            A("dve", lambda e, h=h: e.tensor_tensor(out=Sb[:, h, c0:512], in0=Sb[:, h, c0:512],
                                                                    in1=nlb[:, h, c0:512], op=ALU.add),
                              reads=[nl_u[si], s_u[h]], writes=[s_u[h]])

                def stage_b2(i):
                    G = geom(i)
                    c0, si, eb, spb, nlb, ab = G["c0"], G["si"], G["eb"], G["spb"], G["nlb"], G["ab"]
                    A("act", lambda e: e.activation(out=ab[:, :, c0:512], in_=rr[:, :, c0:512], func=AF.Exp, scale=-1.0),
                      reads=rr_u, writes=[a_u[si]])
                    if G["diag"]:
                        A("dve", lambda e: e.tensor_tensor(out=ab[:, :, c0:c0 + 128], in0=ab[:, :, c0:c0 + 128],
                                                           in1=m1, op=ALU.mult),
                          reads=[a_u[si], cst_u], writes=[a_u[si]])

                def stage_c(i):
                    G = geom(i)
                    c0, si, ab, g, kb_, hp, vv = G["c0"], G["si"], G["ab"], G["g"], G["kb"], G["hp"], G["vv"]
                    bsel = G["grp"] % 2
                    bi = 6 + bsel

                    def av(pe):
                        ins = None
                        for h in range(2):
                            hs = slice(h * 64, (h + 1) * 64)
                            ins = pe.matmul(ps[hs, bi, c0:512], lhsT=vv[:, kb_, hs], rhs=ab[:, h, c0:512],
                                            start=G["first"], stop=(kb_ == 0), skip_group_check=True)
                        return ins
                    A("pe", av, reads=[a_u[si], v_u[hp][G["ktt"]]], writes=[bo_units[bsel]])
                    if kb_ == 0:
                        A("dve", lambda e: e.tensor_copy(out=oT[:, hp, g * TT:(g + 1) * TT], in_=ps[:, bi, :]),
                          reads=[bo_units[bsel]], writes=[o_u[hp][g]])

                for i in range(n + 2):
                    if i < n:
                        stage_a(i)
                    if 0 <= i - 1 < n:
                        stage_b(i - 1)
                    if i < n:
                        stage_a2(i)
                    if 0 <= i - 1 < n:
                        stage_b2(i - 1)
                    if 0 <= i - 2 < n:
                        stage_c(i - 2)
            task([], fna)

    def emit_sgu(l):
        uT = S1
        vtok = S2.rearrange("p c t -> p (c t)").rearrange("p (c f) -> p c f", c=16)
        u_u = [[Unit() for _ in range(NTT)] for _ in range(4)]
        v_u = [Unit() for _ in range(16)]
        emit_branch_out.src_units = u_u
        pbc = t2f(0, 1536)
        sgw_raw = t2f(1536, 1024).rearrange("p (g t) -> p g t", g=8)
        WmT = t2b(2560, 512).rearrange("p (g t) -> p g t", g=8)
        pb_u, wm_u = Unit(), Unit()
        emit_sgu.alias = [wm_u]
        vg_u = [Unit() for _ in range(4)]
        st_u = [Unit() for _ in range(4)]
        tm_u = [Unit(), Unit()]

        def load_params(slots, sus):
            kb.wait_all("sp", ["pe", "act", "dve", "pool"])
            kb.dma("sp", "pb", pbc, pbc_d[l], writes=[pb_u])
            kb.dma("sp", "sw", t2f(1536, 1024), sgw_d[l], writes=[wm_u])
            A("dve", lambda e: e.tensor_tensor(out=WmT, in0=sgw_raw,
                                               in1=M2b.unsqueeze(1).broadcast_to([128, 8, 128]), op=ALU.mult),
              reads=[wm_u, cst_u], writes=[wm_u])
        task([], load_params)

        for j in range(4):
            def fn(slots, sus, j=j):
                w = slots[0][:, 0:1024].rearrange("p (k c) -> p k c", k=8)
                su = sus[0]
                for tt in range(NTT):
                    b = next_bank()
                    mm_group(bank(b), [(w[:, k, :], hT[:, k, 2 + tt * TT:2 + (tt + 1) * TT]) for k in range(8)],
                             reads=[su, h_u[tt]], writes=[bank_u[b]])
                    A("act", lambda e: e.activation(out=uT[:, j, tt * TT:(tt + 1) * TT], in_=bank(b),
                                                    func=AF.Gelu_apprx_tanh),
                      reads=[bank_u[b]], writes=[u_u[j][tt]])
            task([(win_d[l, j], 1024)], fn)

        def fnv(slots, sus):
            ws = [s_[:, 0:1024].rearrange("p (k f) -> p k f", k=2) for s_ in slots]
            for c in range(16):
                k2 = c % 4
                b = next_bank()
                mm_group(bank(b), [(hT[:, k, 2 + c * 128:2 + (c + 1) * 128], ws[k // 2][:, k % 2, :]) for k in range(8)],
                         reads=list(sus) + [h_u[c // 4]], writes=[bank_u[b]])
                vg = t2f((3072 + k2 * 512) if k2 < 2 else (5184 + (k2 - 2) * 512), 512)
                bst = t2f(4096 + k2 * 16, 6)
                mv = t2f(4096 + k2 * 16 + 6, 2)
                rs = t2f(4096 + k2 * 16 + 8, 1)
                A("act", lambda e: e.activation(out=vg, in_=bank(b), func=AF.Gelu_apprx_tanh),
                  reads=[bank_u[b]], writes=[vg_u[k2]])
                A("dve", lambda e: e.bn_stats(out=bst, in_=vg), reads=[vg_u[k2]], writes=[st_u[k2]])
                A("dve", lambda e: e.bn_aggr(out=mv, in_=bst), reads=[st_u[k2]], writes=[st_u[k2]])
                A("dve", lambda e: e.tensor_scalar(out=rs, in0=mv[:, 1:2], scalar1=EPS, scalar2=None, op0=ALU.add),
                  reads=[st_u[k2]], writes=[st_u[k2]])
                A("pool", lambda e: e.tensor_tensor(out=rs, in0=rs, in1=negh[:, :], op=ALU.pow),
                  reads=[st_u[k2], cst_u], writes=[st_u[k2]])
                A("dve", lambda e: e.tensor_scalar(out=vg, in0=vg, scalar1=mv[:, 0:1], scalar2=rs,
                                                   op0=ALU.subtract, op1=ALU.mult),
                  reads=[vg_u[k2], st_u[k2]], writes=[vg_u[k2]])
                A("dve", lambda e: e.tensor_tensor(out=vg, in0=vg, in1=pbc[:, 0:512], op=ALU.mult),
                  reads=[vg_u[k2], pb_u], writes=[vg_u[k2]])
                A("dve", lambda e: e.tensor_tensor(out=vtok[:, c, :], in0=vg, in1=pbc[:, 512:1024], op=ALU.add),
                  reads=[vg_u[k2], pb_u], writes=[v_u[c]])
        task([(wv_d[l, e_], 1024) for e_ in range(4)], fnv)

        def fng(slots, sus):
            sgb = pbc[:, 1024:1536].rearrange("p (g t) -> p g t", g=4)
            for tt in range(NTT):
                for gp in range(4):
                    b = next_bank()

                    def sg(pe):
                        ins = None
                        for ci in range(4):
                            c = tt * 4 + ci
                            for gh in range(2):
                                g = gp * 2 + gh
                                ins = pe.matmul(ps[gh * 64:(gh + 1) * 64, b, ci * 128:(ci + 1) * 128],
                                                lhsT=vtok[:, c, g * 64:(g + 1) * 64], rhs=WmT[:, g, :],
                                                start=True, stop=True)
                        return ins
                    A("pe", sg, reads=[v_u[tt * 4 + ci] for ci in range(4)] + [wm_u], writes=[bank_u[b]])
                    k2 = (tt * 4 + gp) % 2
                    tm = t2f(4160 + k2 * 512, 512)
                    A("dve", lambda e: e.tensor_tensor(
                        out=tm.rearrange("p (c t) -> p c t", c=4), in0=bank(b).rearrange("p (c t) -> p c t", c=4),
                        in1=sgb[:, gp, :].unsqueeze(1).broadcast_to([128, 4, 128]), op=ALU.add),
                      reads=[bank_u[b], pb_u], writes=[tm_u[k2]])
                    cs = slice(tt * TT, (tt + 1) * TT)
                    A("dve", lambda e: e.tensor_tensor(out=uT[:, gp, cs], in0=uT[:, gp, cs], in1=tm, op=ALU.mult),
                      reads=[tm_u[k2], u_u[gp][tt]], writes=[u_u[gp][tt]])
        task([], fng)

    def emit_conv(l):
        c0T = S1
        cT = S2
        c0_u = [[Unit() for _ in range(NTT)] for _ in range(4)]
        c_u = [[Unit() for _ in range(NTT)] for _ in range(4)]
        emit_branch_out.src_units = c_u
        sg_u = [Unit(), Unit()]

        def fpad(slots, sus):
            A("dve", lambda e: e.memset(c0T[:, :, 0:30], 0.0), writes=[c0_u[j][0] for j in range(4)])
        task([], fpad)
        for j in range(4):
            def fn(slots, sus, j=j):
                wp = slots[0][:, 0:1024].rearrange("p (k c) -> p k c", k=8)
                wg = slots[1][:, 0:1024].rearrange("p (k c) -> p k c", k=8)
                sup, sug = sus
                for tt in range(NTT):
                    hcs = slice(2 + tt * TT, 2 + (tt + 1) * TT)
                    bp = next_bank()
                    mm_group(bank(bp), [(wp[:, k, :], hT[:, k, hcs]) for k in range(8)],
                             reads=[sup, h_u[tt]], writes=[bank_u[bp]])
                    bg = next_bank()
                    mm_group(bank(bg), [(wg[:, k, :], hT[:, k, hcs]) for k in range(8)],
                             reads=[sug, h_u[tt]], writes=[bank_u[bg]])
                    k2 = (j * NTT + tt) % 2
                    sg = t2f(5056 + k2 * 512, 512)
                    A("act", lambda e: e.activation(out=sg, in_=bank(bg), func=AF.Sigmoid),
                      reads=[bank_u[bg]], writes=[sg_u[k2]])
                    A("dve", lambda e: e.tensor_tensor(out=c0T[:, j, 30 + tt * TT:30 + (tt + 1) * TT], in0=bank(bp),
                                                       in1=sg, op=ALU.mult),
                      reads=[bank_u[bp], sg_u[k2]], writes=[c0_u[j][tt]])
            task([(win_d[l, 8 + j], 1024), (win_d[l, 12 + j], 1024)], fn)
        task([], lambda slots, sus: kb.barrier())

        diag = t2b(0, 1984).rearrange("p (k c) -> p k c", k=31)
        c1 = t2f(1984, 2048).rearrange("p (j t) -> p j t", j=4)
        sqc = t2b(4032, 1024).rearrange("p (j t) -> p j t", j=4)
        mu = t2f(5056, 512)
        var = t2f(5568, 512)
        msq = t2f(6080, 512)
        dg_u, c1_u, sqc_u, mu_u, var_u, msq_u = [Unit(), Unit()], [Unit() for _ in range(4)], Unit(), Unit(), Unit(), Unit()
        emit_conv.alias = [sqc_u]

        def build_diag(j, hf, k0, k1):
            A("dve", lambda e: e.tensor_tensor(
                out=diag[:, k0:k1, :], in0=ident.unsqueeze(1).broadcast_to([128, k1 - k0, 128]),
                in1=pvec[:, P_CVW + j * 31 + k0:P_CVW + j * 31 + k1].unsqueeze(2).broadcast_to([128, k1 - k0, 128]),
                op=ALU.mult),
              reads=[cst_u, pv_u], writes=[dg_u[hf]])

        def fnc(slots, sus):
            for tt in range(NTT):
                for j in range(4):
                    b = next_bank(0, 4)
                    rd = [c0_u[j][tt]] + ([c0_u[j][tt - 1]] if tt > 0 else [])
                    for hf, (k0, k1) in enumerate(((0, 16), (16, 31))):
                        if not (tt > 0 and j == 0):
                            build_diag(j, hf, k0, k1)

                        def cmm(pe, k0=k0, k1=k1):
                            ins = None
                            for k in range(k0, k1):
                                ins = pe.matmul(bank(b), lhsT=diag[:, k, :], rhs=c0T[:, j, tt * TT + k:tt * TT + k + TT],
                                                start=(k == 0), stop=(k == 30))
                            return ins
                        A("pe", cmm, reads=rd + [dg_u[hf]], writes=[bank_u[b]])
                    cvb = pvec[:, P_CVB + j:P_CVB + j + 1]
                    A("act", lambda e: e.activation(out=c1[:, j, :], in_=bank(b), func=AF.Identity, bias=cvb),
                      reads=[bank_u[b], pv_u], writes=[c1_u[j]])
                    A("act", lambda e: e.activation(out=sqc[:, j, :], in_=bank(b), func=AF.Square, bias=cvb),
                      reads=[bank_u[b], pv_u], writes=[sqc_u])
                bm, bv = 4, 5
                mm_group(bank(bm), [(onesf[:, :], c1[:, j, :]) for j in range(4)],
                         reads=c1_u + [onesf_u], writes=[bank_u[bm]])
                mm_group(bank(bv), [(onesb, sqc[:, j, :]) for j in range(4)],
                         reads=[sqc_u, cst_u], writes=[bank_u[bv]])
                if tt + 1 < NTT:
                    build_diag(0, 0, 0, 16)
                    build_diag(0, 1, 16, 31)
                A("dve", lambda e: e.tensor_scalar(out=mu, in0=bank(bm), scalar1=1.0 / 512, scalar2=None, op0=ALU.mult),
                  reads=[bank_u[bm]], writes=[mu_u])
                A("dve", lambda e: e.tensor_tensor(out=msq, in0=mu, in1=mu, op=ALU.mult),
                  reads=[mu_u], writes=[msq_u])
                A("dve", lambda e: e.scalar_tensor_tensor(out=var, in0=bank(bv), scalar=1.0 / 512, in1=msq,
                                                          op0=ALU.mult, op1=ALU.subtract),
                  reads=[bank_u[bv], msq_u], writes=[var_u])
                A("act", lambda e: e.activation(out=var, in_=var, func=AF.Ln, scale=1.0, bias=EPS),
                  reads=[var_u], writes=[var_u])
                A("act", lambda e: e.activation(out=var, in_=var, func=AF.Exp, scale=-0.5),
                  reads=[var_u], writes=[var_u])
                for j in range(4):
                    A("dve", lambda e, j=j: e.tensor_tensor(out=c1[:, j, :], in0=c1[:, j, :], in1=mu, op=ALU.subtract),
                      reads=[c1_u[j], mu_u], writes=[c1_u[j]])
                    A("dve", lambda e, j=j: e.tensor_tensor(out=c1[:, j, :], in0=c1[:, j, :], in1=var, op=ALU.mult),
                      reads=[c1_u[j], var_u], writes=[c1_u[j]])
                    A("act", lambda e, j=j: e.activation(
                        out=cT[:, j, tt * TT:(tt + 1) * TT], in_=c1[:, j, :], func=AF.Silu,
                        scale=pvec[:, P_CVLG + j:P_CVLG + j + 1], bias=pvec[:, P_CVLB + j:P_CVLB + j + 1]),
                      reads=[c1_u[j], pv_u], writes=[c_u[j][tt]])
        task([], fnc)

    def emit_outproj(l):
        for j in range(8):
            def fn(slots, sus, j=j):
                w = slots[0][:, 0:1024].rearrange("p (k c) -> p k c", k=8)
                su = sus[0]
                for tt in range(NTT):
                    cs = slice(tt * TT, (tt + 1) * TT)
                    b = next_bank()
                    mm_group(bank(b), [(w[:, k, :], merged[:, k, cs]) for k in range(8)],
                             reads=[su, m_u[tt]], writes=[bank_u[b]])
                    A("dve", lambda e: e.tensor_tensor(out=xT[:, j, cs], in0=xT[:, j, cs], in1=bank(b), op=ALU.add),
                      reads=[bank_u[b], x_u[j][tt]], writes=[x_u[j][tt]])
            task([(wout_d[l, j], 1024)], fn)

    def emit_ffn(l):
        a_u = [[Unit() for _ in range(NTT)] for _ in range(11)]
        tg_u = [Unit(), Unit()]
        tv_u = [Unit(), Unit()]

        def a_units(jj, lo, hi):
            return [a_u[jj][i] for i in range(lo // TT, (hi - 1) // TT + 1)]
        for grp in range(2):
            for jj in range(11):
                j = grp * 11 + jj

                def fn(slots, sus, j=j, jj=jj):
                    wg = slots[0][:, 0:1024].rearrange("p (k c) -> p k c", k=8)
                    wv = slots[1][:, 0:1024].rearrange("p (k c) -> p k c", k=8)
                    sug, suv = sus
                    for ti, (st, ln) in enumerate(FFN_TILES):
                        nn = ln + 2
                        k2 = ti % 2
                        hu = h_units(st - 2, st + ln)
                        bg = next_bank()
                        mm_group(ps[:, bg, 0:nn], [(wg[:, k, :], hT[:, k, st:st + nn]) for k in range(8)],
                                 reads=[sug] + hu, writes=[bank_u[bg]])
                        bv = next_bank()
                        mm_group(ps[:, bv, 0:nn], [(wv[:, k, :], hT[:, k, st:st + nn]) for k in range(8)],
                                 reads=[suv] + hu, writes=[bank_u[bv]])
                        tg = t2f(k2 * 1024, 512)[:, 0:ln]
                        tv = t2f(k2 * 1024 + 512, 512)[:, 0:ln]
                        for (bk, tbuf, tu, ch) in ((bg, tg, tg_u[k2], j), (bv, tv, tv_u[k2], 22 + j)):
                            def wcol(k, ch=ch):
                                return pvec[:, P_FW + k * 44 + ch:P_FW + k * 44 + ch + 1]
                            fb = pvec[:, P_FB + ch:P_FB + ch + 1]
                            A("act", lambda e, bk=bk, tbuf=tbuf, wcol=wcol, fb=fb: e.activation(
                                out=tbuf, in_=ps[:, bk, 2:nn], func=AF.Identity, scale=wcol(2), bias=fb),
                              reads=[bank_u[bk], pv_u], writes=[tu])
                            A("dve", lambda e, bk=bk, tbuf=tbuf, wcol=wcol: e.scalar_tensor_tensor(
                                out=tbuf, in0=ps[:, bk, 1:nn - 1], scalar=wcol(1), in1=tbuf, op0=ALU.mult, op1=ALU.add),
                              reads=[bank_u[bk], pv_u, tu], writes=[tu])
                            A("dve", lambda e, bk=bk, tbuf=tbuf, wcol=wcol: e.scalar_tensor_tensor(
                                out=tbuf, in0=ps[:, bk, 0:nn - 2], scalar=wcol(0), in1=tbuf, op0=ALU.mult, op1=ALU.add),
                              reads=[bank_u[bk], pv_u, tu], writes=[tu])
                        A("act", lambda e: e.activation(out=tg, in_=tg, func=AF.Silu),
                          reads=[tg_u[k2]], writes=[tg_u[k2]])
                        A("dve", lambda e: e.tensor_tensor(out=act_ffn[:, jj, st:st + ln], in0=tg, in1=tv, op=ALU.mult),
                          reads=[tg_u[k2], tv_u[k2]], writes=a_units(jj, st, st + ln))
                task([(wup_d[l, j], 1024), (wup_d[l, 22 + j], 1024)], fn)
            for dj in range(8):
                def fnd(slots, sus, dj=dj):
                    w0 = slots[0][:, 0:768].rearrange("p (k c) -> p k c", k=6)
                    w1 = slots[1][:, 0:640].rearrange("p (k c) -> p k c", k=5)
                    su0, su1 = sus
                    for tt in range(NTT):
                        cs = slice(tt * TT, (tt + 1) * TT)
                        b = next_bank()
                        pairs = [(w0[:, k, :], act_ffn[:, k, cs]) for k in range(6)] + \
                                [(w1[:, k, :], act_ffn[:, 6 + k, cs]) for k in range(5)]
                        mm_group(bank(b), pairs, reads=[su0, su1] + [a_u[k][tt] for k in range(11)],
                                 writes=[bank_u[b]])
                        A("dve", lambda e: e.tensor_tensor(out=xT[:, dj, cs], in0=xT[:, dj, cs], in1=bank(b), op=ALU.add),
                          reads=[bank_u[b], x_u[dj][tt]], writes=[x_u[dj][tt]])
                task([(wdn_d[l, grp, dj][:, 0:768], 768), (wdn_d[l, grp, dj][:, 768:1408], 640)], fnd)

    def emit_barrier():
        task([], lambda slots, sus: kb.barrier())

    for l in range(NL):
        task([], lambda slots, sus, l=l: kb.dma("sp", "pv", pvec[:, :], pvec_d[l], writes=[pv_u]))
        emit_rmsnorm(l, P_LN1G)
        emit_barrier()
        emit_attention(l)
        emit_barrier()
        emit_branch_out(l, 2, S2, True, 0)
        emit_barrier()
        emit_sgu(l)
        emit_branch_out(l, 0, S1[:, :, 0:T], False, 1536, alias=emit_sgu.alias)
        emit_barrier()
        emit_conv(l)
        emit_branch_out(l, 1, S2, False, 4032, alias=emit_conv.alias)
        emit_outproj(l)
        emit_barrier()
        emit_rmsnorm(l, P_LN2G)
        emit_barrier()
        emit_ffn(l)
        emit_barrier()

    def store(slots, sus):
        for c in range(8):
            kb.dma("sp", "st", yT_d[c * 128:(c + 1) * 128, :], xT[:, c, :], reads=x_u[c])
        kb.wait_all("sp", ["st"])
    task([], store)

    ent = []
    first_ent = []
    for (es, fn) in tasks:
        first_ent.append(len(ent))
        ent.extend(es)
    next_dma = [0]

    def issue(m):
        s = m % NSLOT
        ap, n = ent[m]
        kb.dma("pool", "w%d" % s, ring[:, s, 0:n], ap, writes=[slot_u[s]])

    LOOK = NSLOT
    for ti, (es, fn) in enumerate(tasks):
        f0 = first_ent[ti]
        last = f0 + len(es) - 1
        assert len(es) <= NSLOT
        while next_dma[0] < len(ent) and next_dma[0] - NSLOT < f0 and next_dma[0] <= max(last, f0 + LOOK - 1):
            issue(next_dma[0])
            next_dma[0] += 1
        assert next_dma[0] > last
        fn([ring[:, (f0 + i) % NSLOT, :] for i in range(len(es))],
           [slot_u[(f0 + i) % NSLOT] for i in range(len(es))])
    return nc


def _consts():
    c = np.zeros((128, NC_), np.float32)
    i = np.arange(128)
    c[:, C_ID:C_ID + 128] = np.eye(128)
    c[:, C_U:C_U + 128] = (i[:, None] > i[None, :])
    c[:, C_M2:C_M2 + 128] = (i[:, None] <= i[None, :])
    c[:, C_M1:C_M1 + 128] = (i[:, None] < i[None, :])
    c[:, C_ONE:C_ONE + 128] = 1.0
    c[:, C_BD:C_BD + 128] = ((i[:, None] // 64) == (i[None, :] // 64))
    return c


def _chunked(w, kc):
    ncol = w.shape[1]
    return np.ascontiguousarray(
        w.reshape(kc, 128, ncol // 128, 128).transpose(2, 1, 0, 3)).reshape(ncol // 128, 128, kc * 128)


def _prep_layer(inp, l):
    f = lambda a: np.asarray(a, dtype=np.float32)
    w_in = f(inp["w_in"][l])
    out = {}
    out["w_in_c"] = _chunked(w_in, 8)
    wv = w_in[:, 512:1024]
    out["w_v"] = np.ascontiguousarray(wv.reshape(4, 2, 128, 512).transpose(0, 2, 1, 3)).reshape(4, 128, 1024)
    out["w_abc"] = np.stack([_chunked(f(inp[k][l]), 4) for k in ("w_a_out", "w_b_out", "w_c_out")])
    out["w_out_c"] = _chunked(f(inp["w_out"][l]), 8)
    out["w_up_c"] = _chunked(f(inp["w_up"][l]), 8)
    wd = f(inp["w_down"][l])
    out["w_dn"] = np.ascontiguousarray(wd.reshape(2, 11, 128, 8, 128).transpose(0, 3, 2, 1, 4)).reshape(2, 8, 128, 1408)
    pv = np.zeros((128, NP), np.float32)
    pv[:, P_LN1G:P_LN1G + 8] = f(inp["ln1_g"][l]).reshape(8, 128).T
    pv[:, P_LN2G:P_LN2G + 8] = f(inp["ln2_g"][l]).reshape(8, 128).T
    pv[:, P_BG:P_BG + 24] = f(inp["b_gate"][l]).reshape(3, 8, 128).transpose(2, 0, 1).reshape(128, 24)
    pv[:, P_CVB:P_CVB + 4] = f(inp["cv_b"][l]).reshape(4, 128).T
    pv[:, P_CVLG:P_CVLG + 4] = f(inp["cv_ln_g"][l]).reshape(4, 128).T
    pv[:, P_CVLB:P_CVLB + 4] = f(inp["cv_ln_b"][l]).reshape(4, 128).T
    pv[:, P_CVW:P_CVW + 124] = f(inp["cv_w"][l]).reshape(31, 4, 128).transpose(2, 1, 0).reshape(128, 124)
    pv[:, P_QG] = np.tile(f(inp["q_norm_g"][l]), 2)
    pv[:, P_KG] = np.tile(f(inp["k_norm_g"][l]), 2)
    pv[:, P_FW:P_FW + 132] = f(inp["ffn_conv_w"][l]).reshape(3, 44, 128).transpose(2, 0, 1).reshape(128, 132)
    pv[:, P_FB:P_FB + 44] = f(inp["ffn_conv_b"][l]).reshape(44, 128).T
    out["pvec"] = pv
    pb = np.zeros((128, 1536), np.float32)
    pb[:, 0:512] = f(inp["sg_ln_g"][l])[None, :]
    pb[:, 512:1024] = f(inp["sg_ln_b"][l])[None, :]
    sgb = f(inp["sg_b"][l])
    pb[:, 1024:1536] = np.repeat(sgb.reshape(4, 2, 128), 64, axis=1).transpose(1, 0, 2).reshape(128, 512)
    out["pbc"] = pb
    out["sgw"] = np.ascontiguousarray(f(inp["sg_w"][l]).transpose(2, 0, 1)).reshape(128, 1024)
    return out


_PROG = {}


def _get_prog(nl):
    if nl not in _PROG:
        _PROG[nl] = build_program(nl)
    return _PROG[nl]


def kernel(**inputs):
    x = np.asarray(inputs["x"], dtype=np.float32)
    B = x.shape[0]
    layers = [_prep_layer(inputs, l) for l in range(DEPTH)]
    cst = _consts()
    xT = [np.ascontiguousarray(x[b].T) for b in range(B)]
    groups = [list(range(DEPTH))] if FUSED else [[l] for l in range(DEPTH)]
    for grp in groups:
        nc = _get_prog(len(grp))
        wmaps = {k: np.stack([layers[l][k] for l in grp]) for k in layers[0]}
        in_maps = []
        for b in range(B):
            m = {"xT": xT[b], "consts": cst}
            m.update(wmaps)
            in_maps.append(m)
        res = run_bass_kernel_spmd(nc, in_maps, core_ids=list(range(B)))
        xT = [np.asarray(res.results[b]["yT"], dtype=np.float32) for b in range(B)]
    return np.stack([xT[b].T for b in range(B)]).astype(np.float32)
```
